# Optimizing a Trainium2 kernel written in Bass

```python
import math
import jax, jax.numpy as jnp
from jax import lax
import numpy as np

D_MODEL = 2048
BATCH = 4
SEQ = 4096
DEPTH = 2

HEAD_DIM = 128
MOBA_HEADS = 8
MOBA_BLOCK = 256
MOBA_TOPK = 3
MOBA_Q_CHUNK = 32
SB_HEADS = 8
SB_Q_BLOCK = 128
POOL_WINDOWS = (2, 4, 8, 16)
POOL_GROUPS = len(POOL_WINDOWS)
POOL_GROUP = 256
POOL_WIDTH = POOL_GROUP * POOL_GROUPS
MOBA_WIDTH = MOBA_HEADS * HEAD_DIM
SB_WIDTH = SB_HEADS * HEAD_DIM
N_BRANCH = 3
IN_WIDTH = 3 * MOBA_WIDTH + POOL_WIDTH + 3 * SB_WIDTH + N_BRANCH * D_MODEL
D_FF = -(-8 * D_MODEL // (3 * 256)) * 256
REL_BUCKETS = 32
REL_MAX_EXACT = 16
REL_MAX_DIST = 2048
EPS = 1e-6
NEG = -1e30

kernel_name = "hybrid_moba_pool_stickbreak_block"


def rms_norm(x, g):
    xf = x.astype(jnp.float32)
    y = xf * lax.rsqrt(jnp.mean(xf * xf, axis=-1, keepdims=True) + EPS)
    return (y * g.astype(jnp.float32)).astype(x.dtype)


def rel_bucket(dist):
    n = jnp.maximum(dist, 0)
    nf = jnp.maximum(n, 1).astype(jnp.float32)
    large = REL_MAX_EXACT + (jnp.log(nf / REL_MAX_EXACT) / math.log(REL_MAX_DIST / REL_MAX_EXACT)
                             * (REL_BUCKETS - REL_MAX_EXACT)).astype(jnp.int32)
    large = jnp.minimum(large, REL_BUCKETS - 1)
    return jnp.where(n < REL_MAX_EXACT, n, large)


def moba_attention(q, k, v, rel_table):
    B, H, S, Dh = q.shape
    nb = -(-S // MOBA_BLOCK)
    pad = nb * MOBA_BLOCK - S
    kb = jnp.pad(k, ((0, 0), (0, 0), (0, pad), (0, 0))).reshape(B, H, nb, MOBA_BLOCK, Dh)
    vb = jnp.pad(v, ((0, 0), (0, 0), (0, pad), (0, 0))).reshape(B, H, nb, MOBA_BLOCK, Dh)
    kbar = jnp.mean(kb.astype(jnp.float32), axis=3)
    topk = min(MOBA_TOPK, nb)
    scale = Dh ** -0.5
    bi = jnp.arange(B)[:, None, None, None]
    hi = jnp.arange(H)[None, :, None, None]
    hi5 = jnp.arange(H)[None, :, None, None, None]
    blk_off = jnp.arange(MOBA_BLOCK)
    Q = MOBA_Q_CHUNK

    def chunk(c):
        t0 = c * Q
        qc = lax.dynamic_slice_in_dim(q, t0, Q, axis=2).astype(jnp.float32)
        pos = t0 + jnp.arange(Q)
        own = t0 // MOBA_BLOCK
        gate = jnp.einsum('bhqd,bhnd->bhqn', qc, kbar)
        gate = jnp.where(jnp.arange(nb) < own, gate, -jnp.inf)
        _, idx = lax.top_k(gate, topk)
        valid = jnp.arange(topk) < own
        ksel = kb[bi, hi, idx].astype(jnp.float32)
        vsel = vb[bi, hi, idx].astype(jnp.float32)
        s_sel = jnp.einsum('bhqd,bhqnkd->bhqnk', qc, ksel) * scale
        key_pos = idx[..., None] * MOBA_BLOCK + blk_off
        bias_sel = rel_table[hi5, rel_bucket(pos[:, None, None] - key_pos)].astype(jnp.float32)
        s_sel = jnp.where(valid[:, None], s_sel + bias_sel, NEG)
        k_own = lax.dynamic_index_in_dim(kb, own, axis=2, keepdims=False).astype(jnp.float32)
        v_own = lax.dynamic_index_in_dim(vb, own, axis=2, keepdims=False).astype(jnp.float32)
        d_own = pos[:, None] - (own * MOBA_BLOCK + blk_off)[None, :]
        s_own = jnp.einsum('bhqd,bhkd->bhqk', qc, k_own) * scale
        s_own = jnp.where(d_own >= 0, s_own + rel_table[:, rel_bucket(d_own)].astype(jnp.float32)[None], NEG)
        logits = jnp.concatenate([s_sel.reshape(B, H, Q, topk * MOBA_BLOCK), s_own], axis=-1)
        p = jax.nn.softmax(logits, axis=-1)
        p_sel = p[..., :topk * MOBA_BLOCK].reshape(B, H, Q, topk, MOBA_BLOCK)
        p_own = p[..., topk * MOBA_BLOCK:]
        return (jnp.einsum('bhqnk,bhqnkd->bhqd', p_sel, vsel)
                + jnp.einsum('bhqk,bhkd->bhqd', p_own, v_own))

    out = lax.map(chunk, jnp.arange(S // Q))
    return jnp.moveaxis(out, 0, 2).reshape(B, H, S, Dh).astype(q.dtype)


def stick_breaking_attention(q, k, v):
    B, H, S, Dh = q.shape
    scale = Dh ** -0.5
    outs = []
    for t0 in range(0, S, SB_Q_BLOCK):
        t1 = t0 + SB_Q_BLOCK
        qc = q[:, :, t0:t1].astype(jnp.float32)
        kc = k[:, :, :t1].astype(jnp.float32)
        vc = v[:, :, :t1].astype(jnp.float32)
        z = jnp.einsum('bhqd,bhkd->bhqk', qc, kc) * scale
        causal = jnp.arange(t1)[None, :] < jnp.arange(t0, t1)[:, None]
        log_beta = jax.nn.log_sigmoid(z)
        log_1m = jnp.where(causal, jax.nn.log_sigmoid(-z), 0.0)
        after = lax.cumsum(log_1m, axis=3, reverse=True) - log_1m
        a = jnp.where(causal, jnp.exp(log_beta + after), 0.0)
        outs.append(jnp.einsum('bhqk,bhkd->bhqd', a, vc))
    return jnp.concatenate(outs, axis=2).astype(q.dtype)


def multiscale_pool(u, w_grp, scale):
    B, S, _ = u.shape
    ug = u.astype(jnp.float32).reshape(B, S, POOL_GROUPS, POOL_GROUP)
    cs = jnp.concatenate([jnp.zeros((B, 1, POOL_GROUPS, POOL_GROUP), jnp.float32),
                          jnp.cumsum(ug, axis=1)], axis=1)
    t = jnp.arange(S)
    pooled = []
    for gi, w in enumerate(POOL_WINDOWS):
        lo = jnp.maximum(t + 1 - w, 0)
        cnt = (t + 1 - lo).astype(jnp.float32)
        mean = (cs[:, 1:, gi] - cs[:, lo, gi]) / cnt[:, None]
        pooled.append(mean - ug[:, :, gi])
    p = jnp.stack(pooled, axis=2)
    y = jnp.einsum('bsgc,gcd->bsgd', p, w_grp.astype(jnp.float32)).reshape(B, S, POOL_WIDTH)
    return (y * scale.astype(jnp.float32)).astype(u.dtype)


def hybrid_mixer(h, w_in, w_pool, pool_scale, w_br_a, w_br_b, w_br_c, w_out, rel_table):
    B, S, _ = h.shape
    proj = h @ w_in
    offs = np.cumsum([MOBA_WIDTH, MOBA_WIDTH, MOBA_WIDTH, POOL_WIDTH, SB_WIDTH, SB_WIDTH, SB_WIDTH]).tolist()
    qa, ka, va, u, qs, ks, vs, gates = jnp.split(proj, offs, axis=-1)
    to_heads = lambda t, nh: t.reshape(B, S, nh, HEAD_DIM).transpose(0, 2, 1, 3)
    from_heads = lambda t: t.transpose(0, 2, 1, 3).reshape(B, S, -1)
    ya = from_heads(moba_attention(to_heads(qa, MOBA_HEADS), to_heads(ka, MOBA_HEADS),
                                   to_heads(va, MOBA_HEADS), rel_table))
    yb = multiscale_pool(u, w_pool, pool_scale)
    yc = from_heads(stick_breaking_attention(to_heads(qs, SB_HEADS), to_heads(ks, SB_HEADS),
                                             to_heads(vs, SB_HEADS)))
    g = jax.nn.sigmoid(gates.astype(jnp.float32)).reshape(B, S, N_BRANCH, D_MODEL)
    m = (g[:, :, 0] * (ya @ w_br_a).astype(jnp.float32)
         + g[:, :, 1] * (yb @ w_br_b).astype(jnp.float32)
         + g[:, :, 2] * (yc @ w_br_c).astype(jnp.float32))
    return m.astype(h.dtype) @ w_out


def swiglu(h, w_gate, w_up, w_down):
    return (jax.nn.silu(h @ w_gate) * (h @ w_up)) @ w_down


def setup_inputs(seed: int = 0) -> dict:
    key = jax.random.key(seed)
    ks = jax.random.split(key, 16)
    nrm = lambda k, shape, fan: jax.random.normal(k, shape, jnp.float32) * (fan ** -0.5)
    return {
        "x": jax.random.normal(ks[0], (BATCH, SEQ, D_MODEL), jnp.float32),
        "norm_mix": 1.0 + 0.02 * jax.random.normal(ks[1], (DEPTH, D_MODEL), jnp.float32),
        "norm_ffn": 1.0 + 0.02 * jax.random.normal(ks[2], (DEPTH, D_MODEL), jnp.float32),
        "w_in": nrm(ks[3], (DEPTH, D_MODEL, IN_WIDTH), D_MODEL),
        "w_pool": nrm(ks[4], (DEPTH, POOL_GROUPS, POOL_GROUP, POOL_GROUP), POOL_GROUP),
        "pool_scale": 1.0 + 0.02 * jax.random.normal(ks[5], (DEPTH, POOL_WIDTH), jnp.float32),
        "w_br_a": nrm(ks[6], (DEPTH, MOBA_WIDTH, D_MODEL), MOBA_WIDTH),
        "w_br_b": nrm(ks[7], (DEPTH, POOL_WIDTH, D_MODEL), POOL_WIDTH),
        "w_br_c": nrm(ks[8], (DEPTH, SB_WIDTH, D_MODEL), SB_WIDTH),
        "w_out": nrm(ks[9], (DEPTH, D_MODEL, D_MODEL), D_MODEL),
        "w_gate": nrm(ks[10], (DEPTH, D_MODEL, D_FF), D_MODEL),
        "w_up": nrm(ks[11], (DEPTH, D_MODEL, D_FF), D_MODEL),
        "w_down": nrm(ks[12], (DEPTH, D_FF, D_MODEL), D_FF),
        "rel_bias": 0.5 * jax.random.normal(ks[13], (MOBA_HEADS, REL_BUCKETS), jnp.float32),
        "norm_final": 1.0 + 0.02 * jax.random.normal(ks[14], (D_MODEL,), jnp.float32),
    }


def reference(x, norm_mix, norm_ffn, w_in, w_pool, pool_scale, w_br_a, w_br_b, w_br_c,
              w_out, w_gate, w_up, w_down, rel_bias, norm_final):
    h = x
    for l in range(DEPTH):
        h = h + hybrid_mixer(rms_norm(h, norm_mix[l]), w_in[l], w_pool[l], pool_scale[l],
                             w_br_a[l], w_br_b[l], w_br_c[l], w_out[l], rel_bias)
        h = h + swiglu(rms_norm(h, norm_ffn[l]), w_gate[l], w_up[l], w_down[l])
    return rms_norm(h, norm_final)
```

```python
import math
import contextlib
import numpy as np
import ml_dtypes
import concourse.bass as bass
import concourse.mybir as mybir
from concourse.bass_utils import run_bass_kernel_spmd

F32 = mybir.dt.float32
BF16 = mybir.dt.bfloat16
AF = mybir.ActivationFunctionType
ALU = mybir.AluOpType
AX = mybir.AxisListType

D = 2048
KC = 16
T = 2048
CH = 512
INW = 13312
DFF = 5632
FC = 44
EPS = 1e-6
SCALE = 128 ** -0.5
NEAR = 20
TKW = 128 * (NEAR - 1) + 512
MKN = 8
MKW = 128 * (MKN - 1) + 512
NEGV = -30000.0
NRING = 6
FUSED = False


class Res:
    __slots__ = ("name", "w", "wm", "rd", "drd", "alias", "sem")

    def __init__(self, name):
        self.name = name
        self.w = None
        self.wm = {}
        self.rd = {}
        self.drd = []
        self.alias = []
        self.sem = None


class Fw:
    def __init__(self, nc):
        self.nc = nc
        self.E = {"pe": nc.tensor, "act": nc.scalar, "dve": nc.vector, "pool": nc.gpsimd, "sp": nc.sync}
        self.csem = {}
        for e in ("pe", "act", "dve", "pool"):
            self.csem[e] = ("c_" + e, nc.alloc_semaphore(name="c_" + e))
        self.cnt = {e: 0 for e in self.csem}
        self.known = {e: {} for e in self.E}
        self.dsems = []
        self.free = []

    def _wait(self, e, tok):
        key, h, val = tok
        if e == "pe" and key == "c_pe":
            return
        if self.known[e].get(key, 0) >= val:
            return
        self.E[e].wait_ge(h, val)
        self.known[e][key] = val

    def _deps(self, reads, writes):
        toks = []
        for r in reads:
            if r.w is not None:
                toks.append(r.w)
            toks.extend(r.wm.values())
        for w0 in writes:
            for w in [w0] + w0.alias:
                if w.w is not None:
                    toks.append(w.w)
                toks.extend(w.wm.values())
                toks.extend(w.rd.values())
                toks.extend(w.drd)
        return toks

    def op(self, e, fn, reads=(), writes=(), inc=True):
        for t in self._deps(reads, writes):
            self._wait(e, t)
        inst = fn()
        key, h = self.csem[e]
        if inc:
            self.cnt[e] += 1
            inst.then_inc(h, 1)
            tok = (key, h, self.cnt[e])
        else:
            tok = (key, h, self.cnt[e] + 1)
        for r in reads:
            r.rd[e] = tok
        for w in writes:
            w.w = tok
            w.wm = {}
            w.rd = {}
            w.drd = []
        return inst

    def _getsem(self, res):
        if res.sem is None:
            if self.free:
                res.sem = self.free.pop()
            else:
                n = len(self.dsems)
                sm = ["d%d" % n, self.nc.alloc_semaphore(name="d%d" % n), 0]
                self.dsems.append(sm)
                res.sem = sm
        return res.sem

    def retire(self, res_list):
        for r in res_list:
            if r.sem is not None:
                self.free.append(r.sem)
                r.sem = None

    def dma(self, q, out, in_, reads=(), writes=(), semres=None, dram_w=None):
        rr = list(reads)
        for t in self._deps(rr, list(writes)):
            self._wait(q, t)
        inst = self.E[q].dma_start(out=out, in_=in_)
        target = semres if semres is not None else (writes[0] if writes else reads[0])
        sm = self._getsem(target)
        sm[2] += 16
        inst.then_inc(sm[1], 16)
        tok = (sm[0], sm[1], sm[2])
        for w in writes:
            w.w = tok
            w.wm = {}
            w.rd = {}
            w.drd = []
        for r in reads:
            r.drd.append(tok)
        if dram_w is not None:
            dram_w.wm[sm[0]] = tok
        return inst

    def barrier(self, engines=("pe", "act", "dve", "pool", "sp")):
        toks = []
        for e in self.csem:
            if self.cnt[e] > 0:
                toks.append((self.csem[e][0], self.csem[e][1], self.cnt[e]))
        for sm in self.dsems:
            if sm[2] > 0:
                toks.append((sm[0], sm[1], sm[2]))
        for e in engines:
            for t in toks:
                self._wait(e, t)


class Ring:
    def __init__(self, items):
        self.items = items
        self.i = 0

    def next(self):
        it = self.items[self.i % len(self.items)]
        self.i += 1
        return it


class Prog:
    def __init__(self, nc):
        self.nc = nc
        self.fw = Fw(nc)
        self.stack = None
        self.allres = []

    def sb(self, name, shape, dt):
        self.uid = getattr(self, "uid", 0) + 1
        return self.stack.enter_context(self.nc.sbuf_tensor("s%d_%s" % (self.uid, name), shape, dt))

    def ps(self, name, shape, dt):
        self.uid = getattr(self, "uid", 0) + 1
        return self.stack.enter_context(self.nc.psum_tensor("p%d_%s" % (self.uid, name), shape, dt))

    def res(self, name):
        r = Res(name)
        self.allres.append(r)
        return r

    def sbr(self, name, shape, dt):
        return (self.sb(name, shape, dt), self.res(name))

    def ring(self, name, n, shape, dt):
        return Ring([self.sbr("%s%d" % (name, i), shape, dt) for i in range(n)])

    def end_phase(self):
        self.fw.barrier()
        self.fw.retire(self.allres)
        self.allres = []


class WStream:
    def __init__(self, P, slots, plist, q="pool"):
        self.P = P
        self.q = q
        self.slots = slots
        self.plist = plist
        self.issued = 0
        self.done = 0
        self.taken = 0

    def _prefetch(self):
        R = len(self.slots)
        while self.issued < len(self.plist) and self.issued - R < self.done:
            n = self.issued
            w_ap, k0, nk, c0, ncols = self.plist[n]
            slot, res = self.slots[n % R]
            src = w_ap[k0 * 128:(k0 + nk) * 128, c0:c0 + ncols].rearrange("(k p) c -> p k c", p=128)
            self.P.fw.dma(self.q, out=slot[:, 0:nk, 0:ncols], in_=src, writes=[res])
            self.issued += 1

    def take(self, n):
        self._prefetch()
        out = []
        for i in range(self.taken, self.taken + n):
            assert i < self.issued, "weight ring too small for this consumer group"
            out.append(self.slots[i % len(self.slots)])
        self.taken += n
        return out

    def release(self, n):
        self.done += n
        self._prefetch()


def mm_group(P, bank, bres, terms):
    nc = P.nc
    n = len(terms)
    for i, (l, r, rd) in enumerate(terms):
        P.fw.op("pe", lambda l=l, r=r, i=i: nc.tensor.matmul(bank, l, r, start=(i == 0), stop=(i == n - 1)),
                reads=rd, writes=[bres], inc=(i == n - 1))


def build_A(P, io, do_q=True, do_kv=True):
    nc, fw = P.nc, P.fw
    with contextlib.ExitStack() as st:
        P.stack = st
        xT_sb = P.sb("xT_sb", [128, KC, T], BF16)
        xres = [P.res("xT%d" % t) for t in range(16)]
        gm, gmr = P.sbr("gm", [128, D], F32)
        ident, identr = P.sbr("ident", [128, 128], BF16)
        fw.dma("sp", out=gm[:], in_=io["gmix"], writes=[gmr])
        fw.dma("sp", out=ident[:], in_=io["ident"], writes=[identr])
        ht = P.ring("ht", 2, [128, D], F32)
        xn = P.ring("xn", 2, [128, D], BF16)
        junk, junkr = P.sbr("junk", [128, D], BF16)
        stat = P.sb("stat", [128, 64], F32)
        statr = [P.res("stat%d" % t) for t in range(16)]
        psb = Ring([(P.ps("psb%d" % i, [128, 1024], BF16), P.res("psb%d" % i)) for i in range(2)])
        psf = Ring([(P.ps("psf%d" % i, [128, 512], F32), P.res("psf%d" % i)) for i in range(6)])
        wslots = [P.sbr("wr%d" % i, [128, 8, 512], BF16) for i in range(NRING)]
        stg_b = P.ring("stgb", 3, [128, T], BF16)
        stg_f = P.ring("stgf", 2, [128, T], F32)
        stg_v = P.ring("stgv", 3, [128, 512], BF16)
        kb = P.ring("kb", 2, [128, 8], F32)
        kbs = P.ring("kbs", 2, [128, 8], F32)

        for tt in range(16):
            h_t, h_r = ht.next()
            fw.dma("sp", out=h_t[:], in_=io["h"][tt * 128:(tt + 1) * 128, :], writes=[h_r])
            sr = statr[tt]
            c = tt * 4
            fw.op("act", lambda: nc.scalar.activation(out=junk[:], in_=h_t[:], func=AF.Square, accum_out=stat[:, c:c + 1]),
                  reads=[h_r], writes=[junkr, sr])
            fw.op("dve", lambda: nc.vector.tensor_scalar(out=stat[:, c + 1:c + 2], in0=stat[:, c:c + 1], scalar1=1.0 / D,
                                                         scalar2=EPS, op0=ALU.mult, op1=ALU.add), reads=[sr], writes=[sr])
            fw.op("act", lambda: nc.scalar.activation(out=stat[:, c + 2:c + 3], in_=stat[:, c + 1:c + 2], func=AF.Sqrt),
                  reads=[sr], writes=[sr])
            fw.op("dve", lambda: nc.vector.reciprocal(out=stat[:, c + 3:c + 4], in_=stat[:, c + 2:c + 3]), reads=[sr], writes=[sr])
            x_t, x_r = xn.next()
            fw.op("dve", lambda: nc.vector.scalar_tensor_tensor(out=x_t[:], in0=h_t[:], scalar=stat[:, c + 3:c + 4], in1=gm[:],
                                                                op0=ALU.mult, op1=ALU.mult), reads=[h_r, sr, gmr], writes=[x_r])
            for half in range(2):
                pb, pbr = psb.next()
                for j in range(8):
                    k = half * 8 + j
                    fw.op("pe", lambda j=j, k=k: nc.tensor.transpose(pb[:, j * 128:(j + 1) * 128], x_t[:, k * 128:(k + 1) * 128], ident[:]),
                          reads=[x_r, identr], writes=[pbr], inc=(j == 7))
                dst = xT_sb[:, half * 8:(half + 1) * 8, tt * 128:(tt + 1) * 128]
                src = pb[:, :].rearrange("p (a b) -> p a b", b=128)
                if half == 0:
                    fw.op("act", lambda: nc.scalar.copy(out=dst, in_=src), reads=[pbr], writes=[xres[tt]])
                else:
                    fw.op("dve", lambda: nc.vector.tensor_copy(out=dst, in_=src), reads=[pbr], writes=[xres[tt]])
        for kg in range(4 if do_q else 0):
            fw.dma("sp", out=io["xT"][kg * 4:(kg + 1) * 4].rearrange("k p t -> p k t"), in_=xT_sb[:, kg * 4:(kg + 1) * 4, :],
                   reads=xres, semres=xres[kg])

        fm_groups = []
        for g in range(2):
            fm_groups.append((0 + 512 * g, "q", 4 * g))
        for g in range(2):
            fm_groups.append((1024 + 512 * g, "ka", 4 * g))
        for g in range(2):
            fm_groups.append((3072 + 512 * g, "u", 4 * g))
        for g in range(2):
            fm_groups.append((4096 + 512 * g, "q", 8 + 4 * g))
        for g in range(2):
            fm_groups.append((5120 + 512 * g, "ks", 8 + 4 * g))
        tm_groups = [(2048, 0), (2560, 512), (6144, 1024), (6656, 1536)]
        if not do_q:
            fm_groups = [g_ for g_ in fm_groups if g_[1] != "q"]
        if not do_kv:
            fm_groups = [g_ for g_ in fm_groups if g_[1] in ("q", "u")]
            tm_groups = []
        plist = []
        for (c0, kind, base) in fm_groups:
            plist.append((io["w_in"], 0, 8, c0, 512))
            plist.append((io["w_in"], 8, 8, c0, 512))
        for (c0, vc0) in tm_groups:
            plist.append((io["w_in"], 0, 8, c0, 512))
            plist.append((io["w_in"], 8, 8, c0, 512))
        ws = WStream(P, wslots, plist)
        for gi_, (c0, kind, base) in enumerate(fm_groups):
            if gi_ > 0:
                ws.release(2)
            pcs = ws.take(2)
            for cb in range(4):
                idx = base + cb
                if kind == "u":
                    s_t, s_r = stg_f.next()
                else:
                    s_t, s_r = stg_b.next()
                if kind == "ka":
                    kb_t, kb_r = kb.next()
                for ch in range(4):
                    bank, br = psf.next()
                    terms = []
                    for k in range(16):
                        sl, slr = pcs[k // 8]
                        terms.append((sl[:, k % 8, cb * 128:(cb + 1) * 128], xT_sb[:, k, ch * 512:(ch + 1) * 512],
                                      [slr] + xres[ch * 4:(ch + 1) * 4]))
                    mm_group(P, bank[:], br, terms)
                    dst = s_t[:, ch * 512:(ch + 1) * 512]
                    if kind == "q":
                        fw.op("act", lambda: nc.scalar.mul(out=dst, in_=bank[:], mul=SCALE), reads=[br], writes=[s_r])
                    elif kind == "u":
                        fw.op("act", lambda: nc.scalar.copy(out=dst, in_=bank[:]), reads=[br], writes=[s_r])
                    else:
                        fw.op("dve", lambda: nc.vector.tensor_copy(out=dst, in_=bank[:]), reads=[br], writes=[s_r])
                        if kind == "ka":
                            fw.op("dve", lambda: nc.vector.tensor_reduce(out=kb_t[:, 2 * ch:2 * ch + 2],
                                                                         in_=bank[:, :].rearrange("p (a b) -> p a b", b=256),
                                                                         axis=AX.X, op=ALU.add), reads=[br], writes=[kb_r])
                if kind == "q":
                    fw.dma("sp", out=io["qT"][idx], in_=s_t[:], reads=[s_r])
                elif kind in ("ka", "ks"):
                    fw.dma("sp", out=io["kT"][idx], in_=s_t[:], reads=[s_r])
                    if kind == "ka":
                        ks_t, ks_r = kbs.next()
                        fw.op("act", lambda: nc.scalar.mul(out=ks_t[:], in_=kb_t[:], mul=1.0 / 256.0), reads=[kb_r], writes=[ks_r])
                        fw.dma("sp", out=io["kbarT"][idx], in_=ks_t[:], reads=[ks_r])
                else:
                    if do_q:
                        fw.dma("sp", out=io["uT"][idx], in_=s_t[:], reads=[s_r])
                    if do_kv:
                        fw.dma("sp", out=io["utail"][idx].rearrange("p (i c) -> p i c", c=16),
                               in_=s_t[:, :].rearrange("p (i c) -> p i c", c=512)[:, :, 496:512], reads=[s_r])
        tog = 0
        for (c0, vc0) in tm_groups:
            ws.release(2)
            pcs = ws.take(2)
            for tt in range(16):
                bank, br = psf.next()
                terms = []
                for k in range(16):
                    sl, slr = pcs[k // 8]
                    terms.append((xT_sb[:, k, tt * 128:(tt + 1) * 128], sl[:, k % 8, :], [slr, xres[tt]]))
                mm_group(P, bank[:], br, terms)
                s_t, s_r = stg_v.next()
                if tog % 2 == 0:
                    fw.op("act", lambda: nc.scalar.copy(out=s_t[:], in_=bank[:]), reads=[br], writes=[s_r])
                else:
                    fw.op("dve", lambda: nc.vector.tensor_copy(out=s_t[:], in_=bank[:]), reads=[br], writes=[s_r])
                tog += 1
                fw.dma("sp", out=io["vtm"][tt * 128:(tt + 1) * 128, vc0:vc0 + 512], in_=s_t[:], reads=[s_r])
        P.end_phase()
    P.stack = None


def build_B(P, io, last, n_moba=8, n_sb=8, do_pool=True, do_chunk=True, skip0=0, mrange=(0, MKN), convert=None):
    nc, fw = P.nc, P.fw
    ys_res = P.res("ysT_dram")

    with contextlib.ExitStack() as st:
        P.stack = st
        ident, identr = P.sbr("identB", [128, 128], BF16)
        tri, trir = P.sbr("tri", [128, 128], BF16)
        ones, onesr = P.sbr("ones", [128, 128], BF16)
        emat, ematr = P.sbr("emat", [16, 16 * 128], BF16)
        mk, mkr = P.sbr("mk", [128, MKW], BF16)
        negk, negkr = P.sbr("negk", [128, MKW], F32)
        pastneg, pastnegr = P.sbr("pastneg", [128, 256], F32)
        past01, past01r = P.sbr("past01", [128, 256], F32)
        own01, own01r = P.sbr("own01", [128, 256], F32)
        bfar, bfarr = P.sbr("bfar", [128, 8], F32)
        for (t_, r_, src) in ((ident, identr, "ident"), (tri, trir, "tri"), (emat, ematr, "emat"), (mk, mkr, "mk"),
                              (negk, negkr, "negk"), (pastneg, pastnegr, "pastneg"), (past01, past01r, "past01"),
                              (own01, own01r, "own01"), (bfar, bfarr, "bfar")):
            fw.dma("sp", out=t_[:], in_=io[src], writes=[r_])
        fw.op("dve", lambda: nc.vector.memset(ones[:], 1.0), writes=[onesr])
        if convert is not None:
            for ci, (dst_ap, src_ap) in enumerate(convert):
                cres = P.res("cvt%d" % ci)
                nrows = src_ap.shape[0]
                step = 512
                for r0 in range(0, nrows, step):
                    fw.dma("pool", out=dst_ap[r0:r0 + step, :], in_=src_ap[r0:r0 + step, :], semres=cres)

        qt = P.ring("qt", 2, [128, T], BF16)
        ktr = P.ring("kt", 2, [128, 4096], BF16)
        vr = P.ring("vv", 2, [128, 32, 128], BF16)
        tk, tkr = P.sbr("tk", [128, TKW], F32)
        kbf, kbfr = P.sbr("kbf", [128, 16], F32)
        kbb, kbbr = P.sbr("kbb", [128, 16], BF16)
        negt, negtr = P.sbr("negt", [16, T], BF16)
        small = P.ring("small", 2, [128, 64], F32)
        negm = P.ring("negm", 2, [128, 16], BF16)
        tmpf = P.ring("tmpf", 4, [128, 512], F32)
        pbf = P.ring("pbf", 4, [128, 512], BF16)
        e1 = P.ring("e1", 3, [128, 512], F32)
        spf = P.ring("spf", 6, [128, 512], F32)
        lm0 = P.ring("lm0", 3, [128, 512], F32)
        lmb = P.ring("lmb", 4, [128, 512], BF16)
        rbuf = P.ring("rbuf", 2, [128, 512], BF16)
        yst = P.ring("yst", 2, [128, T], BF16)
        rec, recr = P.sbr("rec", [128, 512], F32)
        psf = [(P.ps("psB%d" % i, [128, 512], F32), P.res("psB%d" % i)) for i in range(7)]
        psb, psbr = P.ps("psBb", [128, 1024], BF16), P.res("psBb")
        s_ring = Ring(psf[0:3])
        ot_ring = Ring(psf[3:5])
        rs_ring = Ring(psf[5:7])

        def load_head(h16):
            q_t, q_r = qt.next()
            fw.dma("sp", out=q_t[:], in_=io["qT"][h16], writes=[q_r])
            k_t, k_r = ktr.next()
            v_t, v_r = vr.next()
            for j in range(2):
                fw.dma("sp", out=k_t[:, :].rearrange("d (i j p) -> d i j p", j=2, p=512)[:, :, j, :],
                       in_=io["kT_g"][j, h16].rearrange("d (i p) -> d i p", p=512), writes=[k_r])
                for i in range(4):
                    fw.dma("sp", out=v_t[:, (2 * i + j) * 4:(2 * i + j) * 4 + 4, :],
                           in_=io["vtm_g"][j, i * 512:(i + 1) * 512, h16 * 128:(h16 + 1) * 128].rearrange("(r p) d -> p r d", p=128),
                           writes=[v_r])
            return (q_t, q_r, k_t, k_r, v_t, v_r)

        def run_pipeline(units, stages, lags):
            n = len(units)
            for s_ in range(n + lags[-1]):
                for f_, lg in reversed(list(zip(stages, lags))):
                    u_ = s_ - lg
                    if 0 <= u_ < n:
                        f_(units[u_])

        heads = [("m", h) for h in range(n_moba)] + [("s", h) for h in range(n_sb)]
        hd = {}

        def prefetch(kidx):
            if kidx < len(heads) and kidx not in hd:
                kind, h = heads[kidx]
                hd[kidx] = load_head(h if kind == "m" else 8 + h)

        prefetch(0)

        def make_units(kind):
            units = []
            for kidx, (kd, h) in enumerate(heads):
                if kd != kind:
                    continue
                for i in range(4):
                    nk = 8 * i + 8
                    slot = {}
                    for ap_ in range(skip0, nk):
                        units.append({"k": kidx, "h": h, "i": i, "ap": ap_, "a": nk - 1 - ap_, "nk": nk, "slot": slot,
                                      "first": ap_ == skip0, "last": ap_ == nk - 1,
                                      "fh": (i == 0 and ap_ == skip0), "lh": (i == 3 and ap_ == nk - 1)})
            return units

        def moba_prologue(u):
            h = u["h"]
            q_t, q_r, k_t, k_r, v_t, v_r = hd[u["k"]][0:6]
            fw.dma("sp", out=tk[:], in_=io["TK"][h], writes=[tkr])
            for j in range(2):
                fw.dma("sp", out=kbf[:, :].rearrange("d (i j c) -> d i j c", j=2, c=2)[:, :, j, :],
                       in_=io["kbar_g"][j, h].rearrange("d (i c) -> d i c", c=2), writes=[kbfr])
            fw.op("dve", lambda: nc.vector.tensor_copy(out=kbb[:], in_=kbf[:]), reads=[kbfr], writes=[kbbr])
            for t in range(16):
                gb, gbr = s_ring.next()
                fw.op("pe", lambda: nc.tensor.matmul(gb[:, 0:16], q_t[:, t * 128:(t + 1) * 128], kbb[:], start=True, stop=True),
                      reads=[q_r, kbbr], writes=[gbr])
                sm, smr = small.next()
                fw.op("dve", lambda: nc.vector.tensor_tensor(out=sm[:, 0:16], in0=gb[:, 0:16], in1=pastneg[:, t * 16:(t + 1) * 16], op=ALU.add),
                      reads=[gbr, pastnegr], writes=[smr])
                fw.op("dve", lambda: nc.vector.max(out=sm[:, 16:24], in_=sm[:, 0:16]), reads=[smr], writes=[smr])
                fw.op("dve", lambda: nc.vector.tensor_scalar(out=sm[:, 32:48], in0=sm[:, 0:16], scalar1=sm[:, 18:19], scalar2=None, op0=ALU.is_ge),
                      reads=[smr], writes=[smr])
                fw.op("dve", lambda: nc.vector.tensor_tensor(out=sm[:, 32:48], in0=sm[:, 32:48], in1=past01[:, t * 16:(t + 1) * 16], op=ALU.mult),
                      reads=[smr, past01r], writes=[smr])
                fw.op("dve", lambda: nc.vector.tensor_tensor(out=sm[:, 32:48], in0=sm[:, 32:48], in1=own01[:, t * 16:(t + 1) * 16], op=ALU.add),
                      reads=[smr, own01r], writes=[smr])
                nm, nmr = negm.next()
                fw.op("dve", lambda: nc.vector.tensor_scalar(out=nm[:], in0=sm[:, 32:48], scalar1=-1.0, scalar2=-NEGV, op0=ALU.add, op1=ALU.mult),
                      reads=[smr], writes=[nmr])
                fw.op("pe", lambda: nc.tensor.transpose(psb[0:16, 0:128], nm[:], ident[:]), reads=[nmr, identr], writes=[psbr])
                fw.op("act", lambda: nc.scalar.copy(out=negt[:, t * 128:(t + 1) * 128], in_=psb[0:16, 0:128]), reads=[psbr], writes=[negtr])

        def moba_s1(u):
            if u["fh"]:
                moba_prologue(u)
            q_t, q_r, k_t, k_r, v_t, v_r = hd[u["k"]][0:6]
            i, a, ap_ = u["i"], u["a"], u["ap"]
            sl = u["slot"]
            if u["first"]:
                sl["ot"] = ot_ring.next()
                sl["rs"] = rs_ring.next()
                if i == 0:
                    hd[u["k"]] = hd[u["k"]][0:6] + (yst.next(),)
            qc = q_t[:, i * 512:(i + 1) * 512]
            sbk, sbr_ = s_ring.next()
            mm_group(P, sbk[:], sbr_, [
                (k_t[:, a * 128:(a + 1) * 128], qc, [k_r, q_r]),
                (emat[:, (a // 2) * 128:(a // 2 + 1) * 128], negt[:, i * 512:(i + 1) * 512], [ematr, negtr]),
            ])
            u["sb"] = (sbk, sbr_)
            if ap_ < NEAR:
                tm, tmr = tmpf.next()
                fw.op("dve", lambda: nc.vector.tensor_tensor(out=tm[:], in0=sbk[:], in1=tk[:, 128 * ap_:128 * ap_ + 512], op=ALU.add),
                      reads=[sbr_, tkr], writes=[tmr])
                u["tm"] = (tm, tmr)

        def moba_s2(u):
            h = u["h"]
            ap_ = u["ap"]
            p_t, p_r = pbf.next()
            if ap_ < NEAR:
                tm, tmr = u["tm"]
                fw.op("act", lambda: nc.scalar.activation(out=p_t[:], in_=tm[:], func=AF.Exp), reads=[tmr], writes=[p_r])
            else:
                sbk, sbr_ = u["sb"]
                fw.op("act", lambda: nc.scalar.activation(out=p_t[:], in_=sbk[:], func=AF.Exp, bias=bfar[:, h:h + 1]),
                      reads=[sbr_, bfarr], writes=[p_r])
            u["p"] = (p_t, p_r)

        def moba_s3(u):
            h = u["h"]
            q_t, q_r, k_t, k_r, v_t, v_r, (y_t, y_r) = hd[u["k"]]
            i, a, ap_, nk = u["i"], u["a"], u["ap"], u["nk"]
            sl = u["slot"]
            ot, otr = sl["ot"]
            rs, rsr = sl["rs"]
            p_t, p_r = u["p"]
            fw.op("pe", lambda: nc.tensor.matmul(ot[:], v_t[:, a, :], p_t[:], start=u["first"], stop=u["last"]),
                  reads=[v_r, p_r], writes=[otr])
            fw.op("pe", lambda: nc.tensor.matmul(rs[:], ones[:], p_t[:], start=u["first"], stop=u["last"]),
                  reads=[onesr, p_r], writes=[rsr])
            if u["last"]:
                fw.op("dve", lambda: nc.vector.reciprocal(out=rec[:], in_=rs[:]), reads=[rsr], writes=[recr])
                fw.op("dve", lambda: nc.vector.tensor_tensor(out=y_t[:, i * 512:(i + 1) * 512], in0=ot[:], in1=rec[:], op=ALU.mult),
                      reads=[otr, recr], writes=[y_r])
                if u["lh"]:
                    fw.dma("sp", out=io["ysT"][h], in_=y_t[:], reads=[y_r], dram_w=ys_res)
            if u["fh"]:
                prefetch(u["k"] + 1)

        run_pipeline(make_units("m"), [moba_s1, moba_s2, moba_s3], [0, 2, 3])

        z_ring = Ring([psf[0], psf[1], psf[6]])
        l_ring = Ring([psf[2], psf[5]])

        def sb_s1(u):
            q_t, q_r, k_t, k_r, v_t, v_r = hd[u["k"]][0:6]
            i, a, ap_ = u["i"], u["a"], u["ap"]
            sl = u["slot"]
            if u["first"]:
                sl["ot"] = ot_ring.next()
                sl["r_prev"] = None
                if i == 0:
                    hd[u["k"]] = hd[u["k"]][0:6] + (yst.next(),)
            qc = q_t[:, i * 512:(i + 1) * 512]
            zb, zbr = z_ring.next()
            terms = [(k_t[:, a * 128:(a + 1) * 128], qc, [k_r, q_r])]
            mm_group(P, zb[:], zbr, terms)
            u["z"] = (zb, zbr)

        def sb_s2(u):
            zb, zbr = u["z"]
            e_t, e_r = e1.next()
            fw.op("act", lambda: nc.scalar.activation(out=e_t[:], in_=zb[:], func=AF.Exp, scale=-1.0), reads=[zbr], writes=[e_r])
            sp_t, sp_r = spf.next()
            fw.op("act", lambda: nc.scalar.activation(out=sp_t[:], in_=e_t[:], func=AF.Ln, bias=1.0), reads=[e_r], writes=[sp_r])
            u["sp"] = (sp_t, sp_r)

        def sb_s3(u):
            zb, zbr = u["z"]
            sp_t, sp_r = u["sp"]
            lm_t, lm_r = lmb.next()
            ap_ = u["ap"]
            if mrange[0] <= ap_ < mrange[1]:
                l0, l0r = lm0.next()
                fw.op("dve", lambda: nc.vector.scalar_tensor_tensor(out=l0[:], in0=zb[:], scalar=-1.0, in1=sp_t[:], op0=ALU.mult, op1=ALU.subtract),
                      reads=[zbr, sp_r], writes=[l0r])
                fw.op("dve", lambda: nc.vector.tensor_tensor(out=lm_t[:], in0=l0[:], in1=mk[:, 128 * ap_:128 * ap_ + 512], op=ALU.mult),
                      reads=[l0r, mkr], writes=[lm_r])
            else:
                fw.op("dve", lambda: nc.vector.scalar_tensor_tensor(out=lm_t[:], in0=zb[:], scalar=-1.0, in1=sp_t[:], op0=ALU.mult, op1=ALU.subtract),
                      reads=[zbr, sp_r], writes=[lm_r])
            u["lm"] = (lm_t, lm_r)

        def sb_s4(u):
            sl = u["slot"]
            lm_t, lm_r = u["lm"]
            r_prev = sl["r_prev"]
            lb, lbr = l_ring.next()
            terms = [(tri[:], lm_t[:], [trir, lm_r])]
            if r_prev is not None:
                terms.append((ones[:], r_prev[0][:], [onesr, r_prev[1]]))
            mm_group(P, lb[:], lbr, terms)
            if not u["last"]:
                r_t, r_r = rbuf.next()
                if r_prev is None:
                    fw.op("pool", lambda: nc.gpsimd.tensor_copy(out=r_t[:], in_=lm_t[:]), reads=[lm_r], writes=[r_r])
                else:
                    fw.op("pool", lambda: nc.gpsimd.tensor_tensor(out=r_t[:], in0=r_prev[0][:], in1=lm_t[:], op=ALU.add),
                          reads=[r_prev[1], lm_r], writes=[r_r])
                sl["r_prev"] = (r_t, r_r)
            u["lb"] = (lb, lbr)

        def sb_s5(u):
            sp_t, sp_r = u["sp"]
            lb, lbr = u["lb"]
            tm, tmr = tmpf.next()
            fw.op("dve", lambda: nc.vector.tensor_tensor(out=tm[:], in0=lb[:], in1=sp_t[:], op=ALU.subtract),
                  reads=[lbr, sp_r], writes=[tmr])
            ap_ = u["ap"]
            if mrange[0] <= ap_ < mrange[1]:
                fw.op("dve", lambda: nc.vector.tensor_tensor(out=tm[:], in0=tm[:], in1=negk[:, 128 * ap_:128 * ap_ + 512], op=ALU.add),
                      reads=[tmr, negkr], writes=[tmr])
            u["tm"] = (tm, tmr)

        def sb_s6(u):
            tm, tmr = u["tm"]
            p_t, p_r = pbf.next()
            fw.op("act", lambda: nc.scalar.activation(out=p_t[:], in_=tm[:], func=AF.Exp), reads=[tmr], writes=[p_r])
            u["p"] = (p_t, p_r)

        def sb_s7(u):
            h = u["h"]
            q_t, q_r, k_t, k_r, v_t, v_r, (y_t, y_r) = hd[u["k"]]
            i, a = u["i"], u["a"]
            ot, otr = u["slot"]["ot"]
            p_t, p_r = u["p"]
            fw.op("pe", lambda: nc.tensor.matmul(ot[:], v_t[:, a, :], p_t[:], start=u["first"], stop=u["last"]),
                  reads=[v_r, p_r], writes=[otr])
            if u["last"]:
                fw.op("dve", lambda: nc.vector.tensor_copy(out=y_t[:, i * 512:(i + 1) * 512], in_=ot[:]), reads=[otr], writes=[y_r])
                if u["lh"]:
                    fw.dma("sp", out=io["ysT"][16 + h], in_=y_t[:], reads=[y_r], dram_w=ys_res)
            if u["fh"]:
                prefetch(u["k"] + 1)

        run_pipeline(make_units("s"), [sb_s1, sb_s2, sb_s3, sb_s4, sb_s5, sb_s6, sb_s7], [0, 1, 2, 3, 4, 5, 6])

        wp, wpr = P.sbr("wp", [128, 8, 256], BF16)
        fw.dma("pool", out=wp[:], in_=io["w_pool"].rearrange("g (k p) d -> p (g k) d", p=128), writes=[wpr])
        psc, pscr = P.sbr("psc", [128, 8], F32)
        fw.dma("sp", out=psc[:], in_=io["pscale"], writes=[pscr])
        rc0, rc0r = P.sbr("rc0", [128, 4 * 512], F32)
        fw.dma("sp", out=rc0[:], in_=io["rc0"], writes=[rc0r])
        halow, halowr = P.sbr("halow", [128, 2], F32)
        fw.dma("sp", out=halow[:], in_=io["halow"], writes=[halowr])
        ubuf = P.ring("ubuf", 3, [128, 528], F32)
        hal = P.ring("hal", 2, [128, 32], F32)
        sA = P.ring("sA", 2, [128, 528], F32)
        sB = P.ring("sB", 2, [128, 528], F32)
        pl = P.ring("pl", 4, [128, 512], BF16)
        ystb = P.ring("ystb", 3, [128, 512], BF16)
        for i in range(4 if do_pool else 0):
            for g in range(4):
                w = 2 << g
                pls = []
                for c2 in range(2):
                    cc = 2 * g + c2
                    u_t, u_r = ubuf.next()
                    fw.dma("sp", out=u_t[:, 16:528], in_=io["uT"][cc][:, i * 512:(i + 1) * 512], writes=[u_r])
                    h_t, h_r = hal.next()
                    fw.dma("sp", out=h_t[:, 0:16], in_=io["utail_g"][0, cc][:, i * 16:(i + 1) * 16], writes=[h_r])
                    if i > 0:
                        fw.dma("sp", out=h_t[:, 16:32], in_=io["utail_g"][1, cc][:, (i - 1) * 16:i * 16], writes=[h_r])
                    fw.op("dve", lambda: nc.vector.tensor_scalar(out=u_t[:, 0:16], in0=h_t[:, 0:16], scalar1=halow[:, 0:1], scalar2=None, op0=ALU.mult),
                          reads=[h_r, halowr, u_r], writes=[u_r])
                    if i > 0:
                        fw.op("dve", lambda: nc.vector.scalar_tensor_tensor(out=u_t[:, 0:16], in0=h_t[:, 16:32], scalar=halow[:, 1:2], in1=u_t[:, 0:16],
                                                                            op0=ALU.mult, op1=ALU.add), reads=[h_r, halowr, u_r], writes=[u_r])
                    a_t, a_r = sA.next()
                    b_t, b_r = sB.next()
                    fw.op("pool", lambda: nc.gpsimd.tensor_tensor(out=a_t[:, 1:528], in0=u_t[:, 1:528], in1=u_t[:, 0:527], op=ALU.add),
                          reads=[u_r], writes=[a_r])
                    cur, curr, oth, othr = a_t, a_r, b_t, b_r
                    sh = 2
                    lo = 1
                    while sh < w:
                        lo2 = lo + sh
                        fw.op("pool", lambda cur=cur, oth=oth, sh=sh, lo2=lo2: nc.gpsimd.tensor_tensor(
                            out=oth[:, lo2:528], in0=cur[:, lo2:528], in1=cur[:, lo2 - sh:528 - sh], op=ALU.add), reads=[curr], writes=[othr])
                        cur, curr, oth, othr = oth, othr, cur, curr
                        lo = lo2
                        sh *= 2
                    p_t, p_r = pl.next()
                    if i == 0:
                        fw.op("dve", lambda cur=cur: nc.vector.tensor_tensor(out=cur[:, 16:528], in0=cur[:, 16:528], in1=rc0[:, g * 512:(g + 1) * 512], op=ALU.mult),
                              reads=[curr, rc0r], writes=[curr])
                        fw.op("dve", lambda cur=cur: nc.vector.tensor_tensor(out=p_t[:], in0=cur[:, 16:528], in1=u_t[:, 16:528], op=ALU.subtract),
                              reads=[curr, u_r], writes=[p_r])
                    else:
                        fw.op("dve", lambda cur=cur: nc.vector.scalar_tensor_tensor(out=p_t[:], in0=cur[:, 16:528], scalar=1.0 / w, in1=u_t[:, 16:528],
                                                                                    op0=ALU.mult, op1=ALU.subtract), reads=[curr, u_r], writes=[p_r])
                    pls.append((p_t, p_r))
                for db in range(2):
                    bank, br = s_ring.next()
                    mm_group(P, bank[:], br, [(wp[:, 2 * g + kk, db * 128:(db + 1) * 128], pls[kk][0][:], [wpr, pls[kk][1]]) for kk in range(2)])
                    yb_t, yb_r = ystb.next()
                    cc = 2 * g + db
                    fw.op("dve", lambda: nc.vector.tensor_scalar(out=yb_t[:], in0=bank[:], scalar1=psc[:, cc:cc + 1], scalar2=None, op0=ALU.mult),
                          reads=[br, pscr], writes=[yb_r])
                    fw.dma("sp", out=io["ysT"][8 + cc][:, i * 512:(i + 1) * 512], in_=yb_t[:], reads=[yb_r], dram_w=ys_res)
        P.end_phase()
    P.stack = None

    if not do_chunk:
        return
    with contextlib.ExitStack() as st:
        P.stack = st
        ident, identr = P.sbr("identC", [128, 128], BF16)
        fw.dma("sp", out=ident[:], in_=io["ident"], writes=[identr])
        gffn, gffnr = P.sbr("gffn", [128, D], F32)
        fw.dma("sp", out=gffn[:], in_=io["gffn"], writes=[gffnr])
        if last:
            gfin, gfinr = P.sbr("gfin", [128, D], F32)
            fw.dma("sp", out=gfin[:], in_=io["gfin"], writes=[gfinr])
        xc = P.sb("xc", [128, KC, 512], BF16)
        xcr = P.res("xc")
        big = P.sb("big", [128, FC, 512], BF16)
        ycr = P.res("yc")
        mcr = [P.res("mc%d" % c) for c in range(16)]
        atr = [P.res("at%d" % c) for c in range(FC)]
        for r_ in [ycr] + mcr:
            r_.alias = list(atr)
        for r_ in atr:
            r_.alias = [ycr] + mcr
        hn = [P.sbr("hn%d" % t, [128, D], F32) for t in range(4)]
        wslots = [P.sbr("wrc%d" % i, [128, 8, 512], BF16) for i in range(NRING)]
        sig = P.ring("sig", 2, [128, 512], F32)
        prod = P.ring("prod", 2, [128, 512], F32)
        macc = [P.sbr("macc%d" % c, [128, 512], F32) for c in range(4)]
        xn, xnr = P.sbr("xnC", [128, D], BF16)
        junk, junkr = P.sbr("junkC", [128, D], BF16)
        stat = P.sb("statC", [128, 64], F32)
        statr = [P.res("statC%d" % t) for t in range(8)]
        ob = P.ring("ob", 1, [128, D], F32) if last else None
        psf = Ring([(P.ps("psC%d" % i, [128, 512], F32), P.res("psC%d" % i)) for i in range(7)])
        psb, psbr = P.ps("psCb", [128, 1024], BF16), P.res("psCb")

        WB = io["wb"]
        plist = []
        for s in range(4):
            for cg in range(4):
                for br_ in range(3):
                    c0 = br_ * 2048 + cg * 512
                    plist.append((WB["g"], 0, 8, c0, 512))
                    plist.append((WB["g"], 8, 8, c0, 512))
                    plist.append((WB["br"][br_], 0, 8, cg * 512, 512))
            for cg in range(4):
                plist.append((WB["out"], 0, 8, cg * 512, 512))
                plist.append((WB["out"], 8, 8, cg * 512, 512))
            for fg in range(11):
                plist.append((WB["gate"], 0, 8, fg * 512, 512))
                plist.append((WB["gate"], 8, 8, fg * 512, 512))
                plist.append((WB["up"], 0, 8, fg * 512, 512))
                plist.append((WB["up"], 8, 8, fg * 512, 512))
            for cg in range(4):
                for pc in range(6):
                    nkp = 8 if pc < 5 else 4
                    plist.append((WB["down"], pc * 8, nkp, cg * 512, 512))
        ws = WStream(P, wslots, plist, q="sp")

        def rmsnorm_tile(src_t, src_r, sidx, gain_t, gain_r, out_t, out_r):
            c = sidx * 4
            sr = statr[sidx]
            fw.op("act", lambda: nc.scalar.activation(out=junk[:], in_=src_t[:], func=AF.Square, accum_out=stat[:, c:c + 1]),
                  reads=[src_r], writes=[junkr, sr])
            fw.op("dve", lambda: nc.vector.tensor_scalar(out=stat[:, c + 1:c + 2], in0=stat[:, c:c + 1], scalar1=1.0 / D, scalar2=EPS,
                                                         op0=ALU.mult, op1=ALU.add), reads=[sr], writes=[sr])
            fw.op("act", lambda: nc.scalar.activation(out=stat[:, c + 2:c + 3], in_=stat[:, c + 1:c + 2], func=AF.Sqrt), reads=[sr], writes=[sr])
            fw.op("dve", lambda: nc.vector.reciprocal(out=stat[:, c + 3:c + 4], in_=stat[:, c + 2:c + 3]), reads=[sr], writes=[sr])
            fw.op("dve", lambda: nc.vector.scalar_tensor_tensor(out=out_t[:], in0=src_t[:], scalar=stat[:, c + 3:c + 4], in1=gain_t[:],
                                                                op0=ALU.mult, op1=ALU.mult), reads=[src_r, sr, gain_r], writes=[out_r])

        for s in range(4):
            tok0 = s * 512
            fw.dma("sp", out=xc[:], in_=io["xT"][:, :, tok0:tok0 + 512].rearrange("k p t -> p k t"), writes=[xcr])
            fw.dma("sp", out=big[:, 0:24, :], in_=io["ysT"][:, :, tok0:tok0 + 512].rearrange("k p t -> p k t"), reads=[ys_res], writes=[ycr], semres=ycr)
            ycr.drd = []
            for tt in range(4):
                fw.dma("sp", out=hn[tt][0][:], in_=io["h"][tok0 + tt * 128:tok0 + (tt + 1) * 128, :], writes=[hn[tt][1]])
            for cg in range(4):
                for br_ in range(3):
                    g0, g1, bw = ws.take(3)
                    for cb in range(4):
                        gbank, gbr = psf.next()
                        terms = []
                        for k in range(16):
                            sl, slr = (g0, g1)[k // 8]
                            terms.append((sl[:, k % 8, cb * 128:(cb + 1) * 128], xc[:, k, :], [slr, xcr]))
                        mm_group(P, gbank[:], gbr, terms)
                        bbank, bbr = psf.next()
                        terms = []
                        for k in range(8):
                            terms.append((bw[0][:, k, cb * 128:(cb + 1) * 128], big[:, br_ * 8 + k, :], [bw[1], ycr]))
                        mm_group(P, bbank[:], bbr, terms)
                        sg, sgr = sig.next()
                        fw.op("act", lambda: nc.scalar.activation(out=sg[:], in_=gbank[:], func=AF.Sigmoid), reads=[gbr], writes=[sgr])
                        mc_t, mc_r = macc[cb]
                        if br_ == 0:
                            fw.op("dve", lambda: nc.vector.tensor_tensor(out=mc_t[:], in0=sg[:], in1=bbank[:], op=ALU.mult),
                                  reads=[sgr, bbr], writes=[mc_r])
                        else:
                            pr, prr = prod.next()
                            fw.op("dve", lambda: nc.vector.tensor_tensor(out=pr[:], in0=sg[:], in1=bbank[:], op=ALU.mult),
                                  reads=[sgr, bbr], writes=[prr])
                            if br_ == 1:
                                fw.op("dve", lambda: nc.vector.tensor_tensor(out=mc_t[:], in0=mc_t[:], in1=pr[:], op=ALU.add),
                                      reads=[mc_r, prr], writes=[mc_r])
                            else:
                                col = cg * 4 + cb
                                fw.op("dve", lambda: nc.vector.tensor_tensor(out=big[:, 24 + col, :], in0=mc_t[:], in1=pr[:], op=ALU.add),
                                      reads=[mc_r, prr], writes=[mcr[col]])
                    ws.release(3)
            for cg in range(4):
                w0, w1 = ws.take(2)
                for tt in range(4):
                    bank, br = psf.next()
                    terms = []
                    for k in range(16):
                        sl, slr = (w0, w1)[k // 8]
                        terms.append((big[:, 24 + k, tt * 128:(tt + 1) * 128], sl[:, k % 8, :], [slr, mcr[k]]))
                    mm_group(P, bank[:], br, terms)
                    h_t, h_r = hn[tt]
                    fw.op("dve", lambda: nc.vector.tensor_tensor(out=h_t[:, cg * 512:(cg + 1) * 512], in0=h_t[:, cg * 512:(cg + 1) * 512], in1=bank[:], op=ALU.add),
                          reads=[br, h_r], writes=[h_r])
                ws.release(2)
            for tt in range(4):
                h_t, h_r = hn[tt]
                rmsnorm_tile(h_t, h_r, tt, gffn, gffnr, xn, xnr)
                for half in range(2):
                    for j in range(8):
                        k = half * 8 + j
                        fw.op("pe", lambda: nc.tensor.transpose(psb[:, j * 128:(j + 1) * 128], xn[:, k * 128:(k + 1) * 128], ident[:]),
                              reads=[xnr, identr], writes=[psbr], inc=(j == 7))
                    dst = xc[:, half * 8:(half + 1) * 8, tt * 128:(tt + 1) * 128]
                    src = psb[:, :].rearrange("p (a b) -> p a b", b=128)
                    if half == 0:
                        fw.op("act", lambda: nc.scalar.copy(out=dst, in_=src), reads=[psbr], writes=[xcr])
                    else:
                        fw.op("dve", lambda: nc.vector.tensor_copy(out=dst, in_=src), reads=[psbr], writes=[xcr])
            for fg in range(11):
                g0, g1, u0, u1 = ws.take(4)
                for fb in range(4):
                    gbank, gbr = psf.next()
                    terms = []
                    for k in range(16):
                        sl, slr = (g0, g1)[k // 8]
                        terms.append((sl[:, k % 8, fb * 128:(fb + 1) * 128], xc[:, k, :], [slr, xcr]))
                    mm_group(P, gbank[:], gbr, terms)
                    ubank, ubr = psf.next()
                    terms = []
                    for k in range(16):
                        sl, slr = (u0, u1)[k // 8]
                        terms.append((sl[:, k % 8, fb * 128:(fb + 1) * 128], xc[:, k, :], [slr, xcr]))
                    mm_group(P, ubank[:], ubr, terms)
                    sg, sgr = sig.next()
                    fw.op("act", lambda: nc.scalar.activation(out=sg[:], in_=gbank[:], func=AF.Silu), reads=[gbr], writes=[sgr])
                    fc = fg * 4 + fb
                    fw.op("dve", lambda: nc.vector.tensor_tensor(out=big[:, fc, :], in0=sg[:], in1=ubank[:], op=ALU.mult),
                          reads=[sgr, ubr], writes=[atr[fc]])
                ws.release(4)
            for cg in range(4):
                banks = [psf.next() for _ in range(4)]
                for pc in range(6):
                    nkp = 8 if pc < 5 else 4
                    if pc > 0:
                        ws.release(1)
                    (sl, slr), = ws.take(1)
                    for tt in range(4):
                        bank, br = banks[tt]
                        for kk in range(nkp):
                            k = pc * 8 + kk
                            fw.op("pe", lambda: nc.tensor.matmul(bank[:], big[:, k, tt * 128:(tt + 1) * 128], sl[:, kk, :],
                                                                 start=(k == 0), stop=(k == FC - 1)),
                                  reads=[slr, atr[k]], writes=[br], inc=(kk == nkp - 1))
                ws.release(1)
                for tt in range(4):
                    bank, br = banks[tt]
                    h_t, h_r = hn[tt]
                    fw.op("dve", lambda: nc.vector.tensor_tensor(out=h_t[:, cg * 512:(cg + 1) * 512], in0=h_t[:, cg * 512:(cg + 1) * 512], in1=bank[:], op=ALU.add),
                          reads=[br, h_r], writes=[h_r])
            for tt in range(4):
                h_t, h_r = hn[tt]
                dst = io["hout"][tok0 + tt * 128:tok0 + (tt + 1) * 128, :]
                if last:
                    o_t, o_r = ob.next()
                    rmsnorm_tile(h_t, h_r, 4 + tt, gfin, gfinr, o_t, o_r)
                    fw.dma("sp", out=dst, in_=o_t[:], reads=[o_r])
                else:
                    fw.dma("sp", out=dst, in_=h_t[:], reads=[h_r])
        P.end_phase()
    P.stack = None


def _bucket(d):
    n = np.maximum(d, 0)
    nf = np.maximum(n, 1).astype(np.float32)
    large = 16 + (np.log(nf / np.float32(16)) / np.float32(math.log(2048 / 16)) * np.float32(16)).astype(np.int32)
    large = np.minimum(large, 31)
    return np.where(n < 16, n, large)


def _core_tables(j, rel_bias):
    c0 = 512 * j - 896
    k = np.arange(128)[:, None]
    m = np.arange(TKW)[None, :]
    d = m - k + c0
    bk = _bucket(d)
    tkt = rel_bias[:, bk]
    tkt = np.where((d >= 0)[None], tkt, np.float32(NEGV)).astype(np.float32)
    m2 = np.arange(MKW)[None, :]
    d2 = m2 - k + c0
    mk = (d2 >= 1).astype(np.float32)
    negk = np.where(d2 >= 1, 0.0, NEGV).astype(np.float32)
    pastneg = np.zeros((16, 16), np.float32)
    past01 = np.zeros((16, 16), np.float32)
    own01 = np.zeros((16, 16), np.float32)
    for t in range(16):
        i, r = t // 4, t % 4
        own = 4 * i + 2 * j + r // 2
        for n in range(16):
            pastneg[t, n] = 0.0 if n < own else -1e30
            past01[t, n] = 1.0 if n < own else 0.0
            own01[t, n] = 1.0 if n == own else 0.0
    bc = lambda a: np.ascontiguousarray(np.broadcast_to(a.reshape(1, -1), (128, a.size))).astype(np.float32)
    rc0 = np.zeros((4, 512), np.float32)
    for g in range(4):
        w = 2 << g
        pos = 512 * j + np.arange(512)
        rc0[g] = 1.0 / np.minimum(pos + 1, w)
    halow = np.array([1.0, 0.0] if j == 1 else [0.0, 1.0], np.float32)
    return {
        "TK": np.ascontiguousarray(tkt),
        "mk": mk.astype(ml_dtypes.bfloat16),
        "negk": negk,
        "pastneg": bc(pastneg), "past01": bc(past01), "own01": bc(own01),
        "bfar": bc(rel_bias[:, 31]),
        "rc0": bc(rc0), "halow": bc(halow),
    }


def _consts():
    ident = np.eye(128, dtype=np.float32).astype(ml_dtypes.bfloat16)
    jj = np.arange(128)[:, None]
    ss = np.arange(128)[None, :]
    tri = (jj > ss).astype(np.float32).astype(ml_dtypes.bfloat16)
    emat = np.zeros((16, 16, 128), np.float32)
    for n in range(16):
        emat[n, n, :] = 1.0
    return {"ident": ident, "tri": tri, "emat": emat.reshape(16, 2048).astype(ml_dtypes.bfloat16)}


def _decl(nc, name, shape, dt, kind):
    return nc.dram_tensor(name, list(shape), dt, kind=kind).ap()


A_IN = {"h": ([T, D], F32), "w_in": ([D, INW], F32), "gmix": ([128, D], F32), "ident": ([128, 128], BF16)}
A_OUT = {"xT": ([KC, 128, T], BF16), "qT": ([16, 128, T], BF16), "kT": ([16, 128, T], BF16), "vtm": ([T, D], BF16),
         "uT": ([8, 128, T], F32), "utail": ([8, 128, 64], F32), "kbarT": ([8, 128, 8], F32)}
B_IN = {"h": ([T, D], F32), "xT": ([KC, 128, T], BF16), "qT": ([16, 128, T], BF16), "uT": ([8, 128, T], F32),
        "kT_g": ([2, 16, 128, T], BF16), "vtm_g": ([2, T, D], BF16), "kbar_g": ([2, 8, 128, 8], F32),
        "utail_g": ([2, 8, 128, 64], F32),
        "w_in": ([D, INW], F32), "w_pool": ([4, 256, 256], F32), "pscale": ([128, 8], F32),
        "w_br_a": ([1024, D], F32), "w_br_b": ([1024, D], F32), "w_br_c": ([1024, D], F32), "w_out": ([D, D], F32),
        "w_gate": ([D, DFF], F32), "w_up": ([D, DFF], F32), "w_down": ([DFF, D], F32),
        "gffn": ([128, D], F32), "gfin": ([128, D], F32),
        "TK": ([8, 128, TKW], F32), "bfar": ([128, 8], F32), "mk": ([128, MKW], BF16), "negk": ([128, MKW], F32),
        "pastneg": ([128, 256], F32), "past01": ([128, 256], F32), "own01": ([128, 256], F32),
        "halow": ([128, 2], F32), "rc0": ([128, 2048], F32),
        "ident": ([128, 128], BF16), "tri": ([128, 128], BF16), "emat": ([16, 2048], BF16)}


def build_blend(P, io):
    nc, fw = P.nc, P.fw
    with contextlib.ExitStack() as st:
        P.stack = st
        sw, swr = P.sbr("selw", [128, 2], F32)
        fw.dma("sp", out=sw[:], in_=io["selw"], writes=[swr])
        ra = P.ring("bla", 2, [128, D], F32)
        rb = P.ring("blb", 2, [128, D], F32)
        ro = P.ring("blo", 2, [128, D], F32)
        for tt in range(16):
            a_t, a_r = ra.next()
            b_t, b_r = rb.next()
            o_t, o_r = ro.next()
            fw.dma("sp", out=a_t[:], in_=io["a"][tt * 128:(tt + 1) * 128, :], writes=[a_r])
            fw.dma("sp", out=b_t[:], in_=io["b"][tt * 128:(tt + 1) * 128, :], writes=[b_r])
            fw.op("dve", lambda: nc.vector.tensor_scalar(out=o_t[:], in0=a_t[:], scalar1=sw[:, 0:1], scalar2=None, op0=ALU.mult),
                  reads=[a_r, swr], writes=[o_r])
            fw.op("dve", lambda: nc.vector.scalar_tensor_tensor(out=o_t[:], in0=b_t[:], scalar=sw[:, 1:2], in1=o_t[:], op0=ALU.mult, op1=ALU.add),
                  reads=[b_r, swr, o_r], writes=[o_r])
            fw.dma("sp", out=io["out"][tt * 128:(tt + 1) * 128, :], in_=o_t[:], reads=[o_r])
        P.end_phase()
    P.stack = None


TAB_KEYS = {"TK": ([8, 128, TKW], F32), "bfar": ([128, 8], F32), "mk": ([128, MKW], BF16), "negk": ([128, MKW], F32),
            "pastneg": ([128, 256], F32), "past01": ([128, 256], F32), "own01": ([128, 256], F32),
            "halow": ([128, 2], F32), "rc0": ([128, 2048], F32)}
W_KEYS = {"w_in": [2, D, INW], "w_pool": [2, 4, 256, 256], "w_br_a": [2, 1024, D], "w_br_b": [2, 1024, D], "w_br_c": [2, 1024, D],
          "w_out": [2, D, D], "w_gate": [2, D, DFF], "w_up": [2, D, DFF], "w_down": [2, DFF, D]}


def _prog_fused():
    nc = bass.Bass("TRN2", target_bir_lowering=False)
    gi = {}
    for k in ("x0", "x1"):
        gi[k] = _decl(nc, k, [T, D], F32, "ExternalInput")
    for k, shp in W_KEYS.items():
        gi[k] = _decl(nc, k, shp, F32, "ExternalInput")
    for k in ("gmix0", "gmix1", "gffn0", "gffn1", "gfin"):
        gi[k] = _decl(nc, k, [128, D], F32, "ExternalInput")
    for k in ("pscale0", "pscale1"):
        gi[k] = _decl(nc, k, [128, 8], F32, "ExternalInput")
    for pre in ("t0_", "t1_", "to_"):
        for k, (shp, dt) in TAB_KEYS.items():
            gi[pre + k] = _decl(nc, pre + k, shp, dt, "ExternalInput")
    gi["selw"] = _decl(nc, "selw", [128, 2], F32, "ExternalInput")
    gi["ident"] = _decl(nc, "ident", [128, 128], BF16, "ExternalInput")
    gi["tri"] = _decl(nc, "tri", [128, 128], BF16, "ExternalInput")
    gi["emat"] = _decl(nc, "emat", [16, 2048], BF16, "ExternalInput")
    hout = _decl(nc, "hout", [T, D], F32, "ExternalOutput")
    xT_s = _decl(nc, "xT_s", [3, KC, 128, T], BF16, "Internal")
    qT_s = _decl(nc, "qT_s", [3, 16, 128, T], BF16, "Internal")
    uT_s = _decl(nc, "uT_s", [3, 8, 128, T], F32, "Internal")
    kT_s = _decl(nc, "kT_s", [2, 16, 128, T], BF16, "Internal")
    vtm_s = _decl(nc, "vtm_s", [2, T, D], BF16, "Internal")
    kbar_s = _decl(nc, "kbar_s", [2, 8, 128, 8], F32, "Internal")
    utail_s = _decl(nc, "utail_s", [2, 8, 128, 64], F32, "Internal")
    ysT = _decl(nc, "ysT_s", [24, 128, T], BF16, "Internal")
    h1_s = _decl(nc, "h1_s", [3, T, D], F32, "Internal")
    WBF = {"g": _decl(nc, "wb_g", [D, 3 * D], BF16, "Internal"),
           "br": _decl(nc, "wb_br", [3, 1024, D], BF16, "Internal"),
           "out": _decl(nc, "wb_out", [D, D], BF16, "Internal"),
           "gate": _decl(nc, "wb_gate", [D, DFF], BF16, "Internal"),
           "up": _decl(nc, "wb_up", [D, DFF], BF16, "Internal"),
           "down": _decl(nc, "wb_down", [DFF, D], BF16, "Internal")}
    P = Prog(nc)

    def cvt_list(l):
        return [(WBF["g"], gi["w_in"][l][:, 7168:INW]),
                (WBF["br"][0], gi["w_br_a"][l]), (WBF["br"][1], gi["w_br_b"][l]), (WBF["br"][2], gi["w_br_c"][l]),
                (WBF["out"], gi["w_out"][l]), (WBF["gate"], gi["w_gate"][l]), (WBF["up"], gi["w_up"][l]),
                (WBF["down"], gi["w_down"][l])]

    def a_io(h, l, slot, kvslot):
        d = {"h": h, "w_in": gi["w_in"][l], "gmix": gi["gmix%d" % l], "ident": gi["ident"],
             "xT": xT_s[slot], "qT": qT_s[slot], "uT": uT_s[slot]}
        if kvslot is not None:
            d.update({"kT": kT_s[kvslot], "vtm": vtm_s[kvslot], "utail": utail_s[kvslot], "kbarT": kbar_s[kvslot]})
        return d

    def b_io(h, l, slot, tabpre, out):
        d = {"h": h, "xT": xT_s[slot], "qT": qT_s[slot], "uT": uT_s[slot], "kT_g": kT_s, "vtm_g": vtm_s,
             "kbar_g": kbar_s, "utail_g": utail_s, "w_in": gi["w_in"][l], "w_pool": gi["w_pool"][l],
             "pscale": gi["pscale%d" % l], "w_br_a": gi["w_br_a"][l], "w_br_b": gi["w_br_b"][l], "w_br_c": gi["w_br_c"][l],
             "w_out": gi["w_out"][l], "w_gate": gi["w_gate"][l], "w_up": gi["w_up"][l], "w_down": gi["w_down"][l],
             "gffn": gi["gffn%d" % l], "gfin": gi["gfin"], "ident": gi["ident"], "tri": gi["tri"], "emat": gi["emat"],
             "hout": out, "ysT": ysT, "wb": WBF}
        for k in TAB_KEYS:
            d[k] = gi[tabpre + k]
        return d

    build_A(P, a_io(gi["x0"], 0, 0, 0))
    build_A(P, a_io(gi["x1"], 0, 1, 1))
    build_B(P, b_io(gi["x0"], 0, 0, "t0_", h1_s[0]), False, skip0=4, mrange=(4, 8), convert=cvt_list(0))
    build_B(P, b_io(gi["x1"], 0, 1, "t1_", h1_s[1]), False, mrange=(0, 4))
    build_A(P, a_io(h1_s[0], 1, 0, 0), do_q=False)
    build_A(P, a_io(h1_s[1], 1, 1, 1), do_q=False)
    build_blend(P, {"a": h1_s[0], "b": h1_s[1], "out": h1_s[2], "selw": gi["selw"]})
    build_A(P, a_io(h1_s[2], 1, 2, None), do_kv=False)
    build_B(P, b_io(h1_s[2], 1, 2, "to_", hout), True, convert=cvt_list(1))
    return nc


def _bcast(v):
    return np.ascontiguousarray(np.broadcast_to(np.asarray(v, np.float32).reshape(1, -1), (128, v.size)))


def kernel(x, norm_mix, norm_ffn, w_in, w_pool, pool_scale, w_br_a, w_br_b, w_br_c,
           w_out, w_gate, w_up, w_down, rel_bias, norm_final):
    f = lambda a: np.ascontiguousarray(np.asarray(a, dtype=np.float32))
    x = f(x)
    norm_mix, norm_ffn, pool_scale, rel_bias, norm_final = f(norm_mix), f(norm_ffn), f(pool_scale), f(rel_bias), f(norm_final)
    wts = {"w_in": f(w_in), "w_pool": f(w_pool), "w_br_a": f(w_br_a), "w_br_b": f(w_br_b), "w_br_c": f(w_br_c),
           "w_out": f(w_out), "w_gate": f(w_gate), "w_up": f(w_up), "w_down": f(w_down)}
    cores = list(range(8))
    consts = _consts()
    tabs = [_core_tables(j, rel_bias) for j in range(2)]
    common = dict(wts)
    common.update(consts)
    for l in range(2):
        common["gmix%d" % l] = _bcast(norm_mix[l])
        common["gffn%d" % l] = _bcast(norm_ffn[l])
        common["pscale%d" % l] = np.ascontiguousarray(pool_scale[l].reshape(8, 128).T)
    common["gfin"] = _bcast(norm_final)
    for r in range(2):
        for k in TAB_KEYS:
            common["t%d_%s" % (r, k)] = tabs[r][k]
    in_maps = []
    for c in cores:
        b, j = c // 2, c % 2
        d = dict(common)
        xb = x[b].reshape(8, 512, D)
        d["x0"] = np.ascontiguousarray(xb[0::2].reshape(T, D))
        d["x1"] = np.ascontiguousarray(xb[1::2].reshape(T, D))
        for k in TAB_KEYS:
            d["to_" + k] = tabs[j][k]
        sel = np.zeros((128, 2), np.float32)
        sel[:, j] = 1.0
        d["selw"] = sel
        in_maps.append(d)
    nc = _prog_fused()
    res = run_bass_kernel_spmd(nc, in_maps, core_ids=cores).results
    out = np.empty((4, 4096, D), np.float32)
    for c in cores:
        b, j = c // 2, c % 2
        out[b].reshape(8, 512, D)[j::2] = np.asarray(res[c]["hout"]).reshape(4, 512, D)
    return out
```

```python
import math
import contextlib
import numpy as np
import ml_dtypes
import concourse.bass as bass
import concourse.mybir as mybir
from concourse.bass_utils import run_bass_kernel_spmd

F32 = mybir.dt.float32
BF16 = mybir.dt.bfloat16
AF = mybir.ActivationFunctionType
ALU = mybir.AluOpType
AX = mybir.AxisListType

D = 2048
KC = 16
T = 2048
CH = 512
INW = 13312
DFF = 5632
FC = 44
EPS = 1e-6
SCALE = 128 ** -0.5
NEAR = 20
TKW = 128 * (NEAR - 1) + 512
MKN = 8
MKW = 128 * (MKN - 1) + 512
NEGV = -30000.0
NRING = 6
FUSED = False


class Res:
    __slots__ = ("name", "w", "wm", "rd", "drd", "alias", "sem")

    def __init__(self, name):
        self.name = name
        self.w = None
        self.wm = {}
        self.rd = {}
        self.drd = []
        self.alias = []
        self.sem = None


class Fw:
    def __init__(self, nc):
        self.nc = nc
        self.E = {"pe": nc.tensor, "act": nc.scalar, "dve": nc.vector, "pool": nc.gpsimd, "sp": nc.sync}
        self.csem = {}
        for e in ("pe", "act", "dve", "pool"):
            self.csem[e] = ("c_" + e, nc.alloc_semaphore(name="c_" + e))
        self.cnt = {e: 0 for e in self.csem}
        self.known = {e: {} for e in self.E}
        self.dsems = []
        self.free = []

    def _wait(self, e, tok):
        key, h, val = tok
        if e == "pe" and key == "c_pe":
            return
        if self.known[e].get(key, 0) >= val:
            return
        self.E[e].wait_ge(h, val)
        self.known[e][key] = val

    def _deps(self, reads, writes):
        toks = []
        for r in reads:
            if r.w is not None:
                toks.append(r.w)
            toks.extend(r.wm.values())
        for w0 in writes:
            for w in [w0] + w0.alias:
                if w.w is not None:
                    toks.append(w.w)
                toks.extend(w.wm.values())
                toks.extend(w.rd.values())
                toks.extend(w.drd)
        best = {}
        for t in toks:
            if t[0] not in best or best[t[0]][2] < t[2]:
                best[t[0]] = t
        return list(best.values())

    def op(self, e, fn, reads=(), writes=(), inc=True):
        for t in self._deps(reads, writes):
            self._wait(e, t)
        inst = fn()
        key, h = self.csem[e]
        if inc:
            self.cnt[e] += 1
            inst.then_inc(h, 1)
            tok = (key, h, self.cnt[e])
        else:
            tok = (key, h, self.cnt[e] + 1)
        for r in reads:
            r.rd[e] = tok
        for w in writes:
            w.w = tok
            w.wm = {}
            w.rd = {}
            w.drd = []
        return inst

    def _getsem(self, res):
        if res.sem is None:
            if self.free:
                res.sem = self.free.pop()
            else:
                n = len(self.dsems)
                sm = ["d%d" % n, self.nc.alloc_semaphore(name="d%d" % n), 0]
                self.dsems.append(sm)
                res.sem = sm
        return res.sem

    def retire(self, res_list):
        for r in res_list:
            if r.sem is not None:
                self.free.append(r.sem)
                r.sem = None

    def dma(self, q, out, in_, reads=(), writes=(), semres=None, dram_w=None):
        rr = list(reads)
        for t in self._deps(rr, list(writes)):
            self._wait(q, t)
        inst = self.E[q].dma_start(out=out, in_=in_)
        target = semres if semres is not None else (writes[0] if writes else reads[0])
        sm = self._getsem(target)
        sm[2] += 16
        inst.then_inc(sm[1], 16)
        tok = (sm[0], sm[1], sm[2])
        for w in writes:
            w.w = tok
            w.wm = {}
            w.rd = {}
            w.drd = []
        for r in reads:
            r.drd.append(tok)
        if dram_w is not None:
            dram_w.wm[sm[0]] = tok
        return inst

    def barrier(self, engines=("pe", "act", "dve", "pool", "sp")):
        toks = []
        for e in self.csem:
            if self.cnt[e] > 0:
                toks.append((self.csem[e][0], self.csem[e][1], self.cnt[e]))
        for sm in self.dsems:
            if sm[2] > 0:
                toks.append((sm[0], sm[1], sm[2]))
        for e in engines:
            for t in toks:
                self._wait(e, t)


class Ring:
    def __init__(self, items):
        self.items = items
        self.i = 0

    def next(self):
        it = self.items[self.i % len(self.items)]
        self.i += 1
        return it


class Prog:
    def __init__(self, nc):
        self.nc = nc
        self.fw = Fw(nc)
        self.stack = None
        self.allres = []

    def sb(self, name, shape, dt):
        self.uid = getattr(self, "uid", 0) + 1
        return self.stack.enter_context(self.nc.sbuf_tensor("s%d_%s" % (self.uid, name), shape, dt))

    def ps(self, name, shape, dt):
        self.uid = getattr(self, "uid", 0) + 1
        return self.stack.enter_context(self.nc.psum_tensor("p%d_%s" % (self.uid, name), shape, dt))

    def res(self, name):
        r = Res(name)
        self.allres.append(r)
        return r

    def sbr(self, name, shape, dt):
        return (self.sb(name, shape, dt), self.res(name))

    def ring(self, name, n, shape, dt):
        return Ring([self.sbr("%s%d" % (name, i), shape, dt) for i in range(n)])

    def end_phase(self):
        self.fw.barrier()
        self.fw.retire(self.allres)
        self.allres = []


class WStream:
    def __init__(self, P, slots, plist):
        self.P = P
        self.slots = slots
        self.plist = plist
        self.issued = 0
        self.done = 0
        self.taken = 0

    def _prefetch(self):
        R = len(self.slots)
        while self.issued < len(self.plist) and self.issued - R < self.done:
            n = self.issued
            w_ap, k0, nk, c0, ncols = self.plist[n]
            slot, res = self.slots[n % R]
            src = w_ap[k0 * 128:(k0 + nk) * 128, c0:c0 + ncols].rearrange("(k p) c -> p k c", p=128)
            self.P.fw.dma("pool", out=slot[:, 0:nk, 0:ncols], in_=src, writes=[res])
            self.issued += 1

    def take(self, n):
        self._prefetch()
        out = []
        for i in range(self.taken, self.taken + n):
            assert i < self.issued, "weight ring too small for this consumer group"
            out.append(self.slots[i % len(self.slots)])
        self.taken += n
        return out

    def release(self, n):
        self.done += n
        self._prefetch()


def mm_group(P, bank, bres, terms):
    nc = P.nc
    n = len(terms)
    for i, (l, r, rd) in enumerate(terms):
        P.fw.op("pe", lambda l=l, r=r, i=i: nc.tensor.matmul(bank, l, r, start=(i == 0), stop=(i == n - 1)),
                reads=rd, writes=[bres], inc=(i == n - 1))


def build_A(P, io, do_q=True, do_kv=True):
    nc, fw = P.nc, P.fw
    with contextlib.ExitStack() as st:
        P.stack = st
        xT_sb = P.sb("xT_sb", [128, KC, T], BF16)
        xres = [P.res("xT%d" % t) for t in range(16)]
        gm, gmr = P.sbr("gm", [128, D], F32)
        ident, identr = P.sbr("ident", [128, 128], BF16)
        fw.dma("sp", out=gm[:], in_=io["gmix"], writes=[gmr])
        fw.dma("sp", out=ident[:], in_=io["ident"], writes=[identr])
        ht = P.ring("ht", 2, [128, D], F32)
        xn = P.ring("xn", 2, [128, D], BF16)
        junk, junkr = P.sbr("junk", [128, D], BF16)
        stat = P.sb("stat", [128, 64], F32)
        statr = [P.res("stat%d" % t) for t in range(16)]
        psb = Ring([(P.ps("psb%d" % i, [128, 1024], BF16), P.res("psb%d" % i)) for i in range(2)])
        psf = Ring([(P.ps("psf%d" % i, [128, 512], F32), P.res("psf%d" % i)) for i in range(6)])
        wslots = [P.sbr("wr%d" % i, [128, 8, 512], BF16) for i in range(NRING)]
        stg_b = P.ring("stgb", 3, [128, T], BF16)
        stg_f = P.ring("stgf", 2, [128, T], F32)
        stg_v = P.ring("stgv", 3, [128, 512], BF16)
        kb = P.ring("kb", 2, [128, 8], F32)
        kbs = P.ring("kbs", 2, [128, 8], F32)

        for tt in range(16):
            h_t, h_r = ht.next()
            fw.dma("sp", out=h_t[:], in_=io["h"][tt * 128:(tt + 1) * 128, :], writes=[h_r])
            sr = statr[tt]
            c = tt * 4
            fw.op("act", lambda: nc.scalar.activation(out=junk[:], in_=h_t[:], func=AF.Square, accum_out=stat[:, c:c + 1]),
                  reads=[h_r], writes=[junkr, sr])
            fw.op("dve", lambda: nc.vector.tensor_scalar(out=stat[:, c + 1:c + 2], in0=stat[:, c:c + 1], scalar1=1.0 / D,
                                                         scalar2=EPS, op0=ALU.mult, op1=ALU.add), reads=[sr], writes=[sr])
            fw.op("act", lambda: nc.scalar.activation(out=stat[:, c + 2:c + 3], in_=stat[:, c + 1:c + 2], func=AF.Sqrt),
                  reads=[sr], writes=[sr])
            fw.op("dve", lambda: nc.vector.reciprocal(out=stat[:, c + 3:c + 4], in_=stat[:, c + 2:c + 3]), reads=[sr], writes=[sr])
            x_t, x_r = xn.next()
            fw.op("dve", lambda: nc.vector.scalar_tensor_tensor(out=x_t[:], in0=h_t[:], scalar=stat[:, c + 3:c + 4], in1=gm[:],
                                                                op0=ALU.mult, op1=ALU.mult), reads=[h_r, sr, gmr], writes=[x_r])
            for half in range(2):
                pb, pbr = psb.next()
                for j in range(8):
                    k = half * 8 + j
                    fw.op("pe", lambda j=j, k=k: nc.tensor.transpose(pb[:, j * 128:(j + 1) * 128], x_t[:, k * 128:(k + 1) * 128], ident[:]),
                          reads=[x_r, identr], writes=[pbr], inc=(j == 7))
                dst = xT_sb[:, half * 8:(half + 1) * 8, tt * 128:(tt + 1) * 128]
                src = pb[:, :].rearrange("p (a b) -> p a b", b=128)
                if half == 0:
                    fw.op("act", lambda: nc.scalar.copy(out=dst, in_=src), reads=[pbr], writes=[xres[tt]])
                else:
                    fw.op("dve", lambda: nc.vector.tensor_copy(out=dst, in_=src), reads=[pbr], writes=[xres[tt]])
        for kg in range(4 if do_q else 0):
            fw.dma("sp", out=io["xT"][kg * 4:(kg + 1) * 4].rearrange("k p t -> p k t"), in_=xT_sb[:, kg * 4:(kg + 1) * 4, :],
                   reads=xres, semres=xres[kg])

        fm_groups = []
        for g in range(2):
            fm_groups.append((0 + 512 * g, "q", 4 * g))
        for g in range(2):
            fm_groups.append((1024 + 512 * g, "ka", 4 * g))
        for g in range(2):
            fm_groups.append((3072 + 512 * g, "u", 4 * g))
        for g in range(2):
            fm_groups.append((4096 + 512 * g, "q", 8 + 4 * g))
        for g in range(2):
            fm_groups.append((5120 + 512 * g, "ks", 8 + 4 * g))
        tm_groups = [(2048, 0), (2560, 512), (6144, 1024), (6656, 1536)]
        if not do_q:
            fm_groups = [g_ for g_ in fm_groups if g_[1] != "q"]
        if not do_kv:
            fm_groups = [g_ for g_ in fm_groups if g_[1] in ("q", "u")]
            tm_groups = []
        plist = []
        for (c0, kind, base) in fm_groups:
            plist.append((io["w_in"], 0, 8, c0, 512))
            plist.append((io["w_in"], 8, 8, c0, 512))
        for (c0, vc0) in tm_groups:
            plist.append((io["w_in"], 0, 8, c0, 512))
            plist.append((io["w_in"], 8, 8, c0, 512))
        ws = WStream(P, wslots, plist)
        for gi_, (c0, kind, base) in enumerate(fm_groups):
            if gi_ > 0:
                ws.release(2)
            pcs = ws.take(2)
            for cb in range(4):
                idx = base + cb
                if kind == "u":
                    s_t, s_r = stg_f.next()
                else:
                    s_t, s_r = stg_b.next()
                if kind == "ka":
                    kb_t, kb_r = kb.next()
                for ch in range(4):
                    bank, br = psf.next()
                    terms = []
                    for k in range(16):
                        sl, slr = pcs[k // 8]
                        terms.append((sl[:, k % 8, cb * 128:(cb + 1) * 128], xT_sb[:, k, ch * 512:(ch + 1) * 512],
                                      [slr] + xres[ch * 4:(ch + 1) * 4]))
                    mm_group(P, bank[:], br, terms)
                    dst = s_t[:, ch * 512:(ch + 1) * 512]
                    if kind == "q":
                        fw.op("act", lambda: nc.scalar.mul(out=dst, in_=bank[:], mul=SCALE), reads=[br], writes=[s_r])
                    elif kind == "u":
                        fw.op("act", lambda: nc.scalar.copy(out=dst, in_=bank[:]), reads=[br], writes=[s_r])
                    else:
                        fw.op("dve", lambda: nc.vector.tensor_copy(out=dst, in_=bank[:]), reads=[br], writes=[s_r])
                        if kind == "ka":
                            fw.op("dve", lambda: nc.vector.tensor_reduce(out=kb_t[:, 2 * ch:2 * ch + 2],
                                                                         in_=bank[:, :].rearrange("p (a b) -> p a b", b=256),
                                                                         axis=AX.X, op=ALU.add), reads=[br], writes=[kb_r])
                if kind == "q":
                    fw.dma("sp", out=io["qT"][idx], in_=s_t[:], reads=[s_r])
                elif kind in ("ka", "ks"):
                    fw.dma("sp", out=io["kT"][idx], in_=s_t[:], reads=[s_r])
                    if kind == "ka":
                        ks_t, ks_r = kbs.next()
                        fw.op("act", lambda: nc.scalar.mul(out=ks_t[:], in_=kb_t[:], mul=1.0 / 256.0), reads=[kb_r], writes=[ks_r])
                        fw.dma("sp", out=io["kbarT"][idx], in_=ks_t[:], reads=[ks_r])
                else:
                    if do_q:
                        fw.dma("sp", out=io["uT"][idx], in_=s_t[:], reads=[s_r])
                    if do_kv:
                        fw.dma("sp", out=io["utail"][idx].rearrange("p (i c) -> p i c", c=16),
                               in_=s_t[:, :].rearrange("p (i c) -> p i c", c=512)[:, :, 496:512], reads=[s_r])
        tog = 0
        for (c0, vc0) in tm_groups:
            ws.release(2)
            pcs = ws.take(2)
            for tt in range(16):
                bank, br = psf.next()
                terms = []
                for k in range(16):
                    sl, slr = pcs[k // 8]
                    terms.append((xT_sb[:, k, tt * 128:(tt + 1) * 128], sl[:, k % 8, :], [slr, xres[tt]]))
                mm_group(P, bank[:], br, terms)
                s_t, s_r = stg_v.next()
                if tog % 2 == 0:
                    fw.op("act", lambda: nc.scalar.copy(out=s_t[:], in_=bank[:]), reads=[br], writes=[s_r])
                else:
                    fw.op("dve", lambda: nc.vector.tensor_copy(out=s_t[:], in_=bank[:]), reads=[br], writes=[s_r])
                tog += 1
                fw.dma("sp", out=io["vtm"][tt * 128:(tt + 1) * 128, vc0:vc0 + 512], in_=s_t[:], reads=[s_r])
        P.end_phase()
    P.stack = None


def build_B(P, io, last, n_moba=8, n_sb=8, do_pool=True, do_chunk=True, skip0=0, mrange=(0, MKN)):
    nc, fw = P.nc, P.fw
    ys_res = P.res("ysT_dram")

    with contextlib.ExitStack() as st:
        P.stack = st
        ident, identr = P.sbr("identB", [128, 128], BF16)
        tri, trir = P.sbr("tri", [128, 128], BF16)
        ones, onesr = P.sbr("ones", [128, 128], BF16)
        emat, ematr = P.sbr("emat", [16, 16 * 128], BF16)
        mk, mkr = P.sbr("mk", [128, MKW], BF16)
        negk, negkr = P.sbr("negk", [128, MKW], F32)
        pastneg, pastnegr = P.sbr("pastneg", [128, 256], F32)
        past01, past01r = P.sbr("past01", [128, 256], F32)
        own01, own01r = P.sbr("own01", [128, 256], F32)
        bfar, bfarr = P.sbr("bfar", [128, 8], F32)
        for (t_, r_, src) in ((ident, identr, "ident"), (tri, trir, "tri"), (emat, ematr, "emat"), (mk, mkr, "mk"),
                              (negk, negkr, "negk"), (pastneg, pastnegr, "pastneg"), (past01, past01r, "past01"),
                              (own01, own01r, "own01"), (bfar, bfarr, "bfar")):
            fw.dma("sp", out=t_[:], in_=io[src], writes=[r_])
        fw.op("dve", lambda: nc.vector.memset(ones[:], 1.0), writes=[onesr])

        qt = P.ring("qt", 2, [128, T], BF16)
        ktr = P.ring("kt", 2, [128, 4096], BF16)
        vr = P.ring("vv", 2, [128, 32, 128], BF16)
        tk, tkr = P.sbr("tk", [128, TKW], F32)
        kbf, kbfr = P.sbr("kbf", [128, 16], F32)
        kbb, kbbr = P.sbr("kbb", [128, 16], BF16)
        negt, negtr = P.sbr("negt", [16, T], BF16)
        small = P.ring("small", 2, [128, 64], F32)
        negm = P.ring("negm", 2, [128, 16], BF16)
        tmpf = P.ring("tmpf", 4, [128, 512], F32)
        pbf = P.ring("pbf", 4, [128, 512], BF16)
        e1 = P.ring("e1", 3, [128, 512], F32)
        spf = P.ring("spf", 6, [128, 512], F32)
        lm0 = P.ring("lm0", 3, [128, 512], F32)
        lmb = P.ring("lmb", 4, [128, 512], BF16)
        rbuf = P.ring("rbuf", 2, [128, 512], BF16)
        yst = P.ring("yst", 2, [128, T], BF16)
        rec, recr = P.sbr("rec", [128, 512], F32)
        psf = [(P.ps("psB%d" % i, [128, 512], F32), P.res("psB%d" % i)) for i in range(7)]
        psb, psbr = P.ps("psBb", [128, 1024], BF16), P.res("psBb")
        s_ring = Ring(psf[0:3])
        ot_ring = Ring(psf[3:5])
        rs_ring = Ring(psf[5:7])

        def load_head(h16):
            q_t, q_r = qt.next()
            fw.dma("sp", out=q_t[:], in_=io["qT"][h16], writes=[q_r])
            k_t, k_r = ktr.next()
            v_t, v_r = vr.next()
            for j in range(2):
                fw.dma("sp", out=k_t[:, :].rearrange("d (i j p) -> d i j p", j=2, p=512)[:, :, j, :],
                       in_=io["kT_g"][j, h16].rearrange("d (i p) -> d i p", p=512), writes=[k_r])
                for i in range(4):
                    fw.dma("sp", out=v_t[:, (2 * i + j) * 4:(2 * i + j) * 4 + 4, :],
                           in_=io["vtm_g"][j, i * 512:(i + 1) * 512, h16 * 128:(h16 + 1) * 128].rearrange("(r p) d -> p r d", p=128),
                           writes=[v_r])
            return (q_t, q_r, k_t, k_r, v_t, v_r)

        def run_pipeline(units, stages, lags):
            n = len(units)
            for s_ in range(n + lags[-1]):
                for f_, lg in reversed(list(zip(stages, lags))):
                    u_ = s_ - lg
                    if 0 <= u_ < n:
                        f_(units[u_])

        heads = [("m", h) for h in range(n_moba)] + [("s", h) for h in range(n_sb)]
        hd = {}

        def prefetch(kidx):
            if kidx < len(heads) and kidx not in hd:
                kind, h = heads[kidx]
                hd[kidx] = load_head(h if kind == "m" else 8 + h)

        prefetch(0)

        def make_units(kind):
            units = []
            for kidx, (kd, h) in enumerate(heads):
                if kd != kind:
                    continue
                for i in range(4):
                    nk = 8 * i + 8
                    slot = {}
                    for ap_ in range(skip0, nk):
                        units.append({"k": kidx, "h": h, "i": i, "ap": ap_, "a": nk - 1 - ap_, "nk": nk, "slot": slot,
                                      "first": ap_ == skip0, "last": ap_ == nk - 1,
                                      "fh": (i == 0 and ap_ == skip0), "lh": (i == 3 and ap_ == nk - 1)})
            return units

        def moba_prologue(u):
            h = u["h"]
            q_t, q_r, k_t, k_r, v_t, v_r = hd[u["k"]][0:6]
            fw.dma("sp", out=tk[:], in_=io["TK"][h], writes=[tkr])
            for j in range(2):
                fw.dma("sp", out=kbf[:, :].rearrange("d (i j c) -> d i j c", j=2, c=2)[:, :, j, :],
                       in_=io["kbar_g"][j, h].rearrange("d (i c) -> d i c", c=2), writes=[kbfr])
            fw.op("dve", lambda: nc.vector.tensor_copy(out=kbb[:], in_=kbf[:]), reads=[kbfr], writes=[kbbr])
            for t in range(16):
                gb, gbr = s_ring.next()
                fw.op("pe", lambda: nc.tensor.matmul(gb[:, 0:16], q_t[:, t * 128:(t + 1) * 128], kbb[:], start=True, stop=True),
                      reads=[q_r, kbbr], writes=[gbr])
                sm, smr = small.next()
                fw.op("dve", lambda: nc.vector.tensor_tensor(out=sm[:, 0:16], in0=gb[:, 0:16], in1=pastneg[:, t * 16:(t + 1) * 16], op=ALU.add),
                      reads=[gbr, pastnegr], writes=[smr])
                fw.op("dve", lambda: nc.vector.max(out=sm[:, 16:24], in_=sm[:, 0:16]), reads=[smr], writes=[smr])
                fw.op("dve", lambda: nc.vector.tensor_scalar(out=sm[:, 32:48], in0=sm[:, 0:16], scalar1=sm[:, 18:19], scalar2=None, op0=ALU.is_ge),
                      reads=[smr], writes=[smr])
                fw.op("dve", lambda: nc.vector.tensor_tensor(out=sm[:, 32:48], in0=sm[:, 32:48], in1=past01[:, t * 16:(t + 1) * 16], op=ALU.mult),
                      reads=[smr, past01r], writes=[smr])
                fw.op("dve", lambda: nc.vector.tensor_tensor(out=sm[:, 32:48], in0=sm[:, 32:48], in1=own01[:, t * 16:(t + 1) * 16], op=ALU.add),
                      reads=[smr, own01r], writes=[smr])
                nm, nmr = negm.next()
                fw.op("dve", lambda: nc.vector.tensor_scalar(out=nm[:], in0=sm[:, 32:48], scalar1=-1.0, scalar2=-NEGV, op0=ALU.add, op1=ALU.mult),
                      reads=[smr], writes=[nmr])
                fw.op("pe", lambda: nc.tensor.transpose(psb[0:16, 0:128], nm[:], ident[:]), reads=[nmr, identr], writes=[psbr])
                fw.op("act", lambda: nc.scalar.copy(out=negt[:, t * 128:(t + 1) * 128], in_=psb[0:16, 0:128]), reads=[psbr], writes=[negtr])

        def moba_s1(u):
            if u["fh"]:
                moba_prologue(u)
            q_t, q_r, k_t, k_r, v_t, v_r = hd[u["k"]][0:6]
            i, a, ap_ = u["i"], u["a"], u["ap"]
            sl = u["slot"]
            if u["first"]:
                sl["ot"] = ot_ring.next()
                sl["rs"] = rs_ring.next()
                if i == 0:
                    hd[u["k"]] = hd[u["k"]][0:6] + (yst.next(),)
            qc = q_t[:, i * 512:(i + 1) * 512]
            sbk, sbr_ = s_ring.next()
            mm_group(P, sbk[:], sbr_, [
                (k_t[:, a * 128:(a + 1) * 128], qc, [k_r, q_r]),
                (emat[:, (a // 2) * 128:(a // 2 + 1) * 128], negt[:, i * 512:(i + 1) * 512], [ematr, negtr]),
            ])
            u["sb"] = (sbk, sbr_)
            if ap_ < NEAR:
                tm, tmr = tmpf.next()
                fw.op("dve", lambda: nc.vector.tensor_tensor(out=tm[:], in0=sbk[:], in1=tk[:, 128 * ap_:128 * ap_ + 512], op=ALU.add),
                      reads=[sbr_, tkr], writes=[tmr])
                u["tm"] = (tm, tmr)

        def moba_s2(u):
            h = u["h"]
            ap_ = u["ap"]
            p_t, p_r = pbf.next()
            if ap_ < NEAR:
                tm, tmr = u["tm"]
                fw.op("act", lambda: nc.scalar.activation(out=p_t[:], in_=tm[:], func=AF.Exp), reads=[tmr], writes=[p_r])
            else:
                sbk, sbr_ = u["sb"]
                fw.op("act", lambda: nc.scalar.activation(out=p_t[:], in_=sbk[:], func=AF.Exp, bias=bfar[:, h:h + 1]),
                      reads=[sbr_, bfarr], writes=[p_r])
            u["p"] = (p_t, p_r)

        def moba_s3(u):
            h = u["h"]
            q_t, q_r, k_t, k_r, v_t, v_r, (y_t, y_r) = hd[u["k"]]
            i, a, ap_, nk = u["i"], u["a"], u["ap"], u["nk"]
            sl = u["slot"]
            ot, otr = sl["ot"]
            rs, rsr = sl["rs"]
            p_t, p_r = u["p"]
            fw.op("pe", lambda: nc.tensor.matmul(ot[:], v_t[:, a, :], p_t[:], start=u["first"], stop=u["last"]),
                  reads=[v_r, p_r], writes=[otr])
            fw.op("pe", lambda: nc.tensor.matmul(rs[:], ones[:], p_t[:], start=u["first"], stop=u["last"]),
                  reads=[onesr, p_r], writes=[rsr])
            if u["last"]:
                fw.op("dve", lambda: nc.vector.reciprocal(out=rec[:], in_=rs[:]), reads=[rsr], writes=[recr])
                fw.op("dve", lambda: nc.vector.tensor_tensor(out=y_t[:, i * 512:(i + 1) * 512], in0=ot[:], in1=rec[:], op=ALU.mult),
                      reads=[otr, recr], writes=[y_r])
                if u["lh"]:
                    fw.dma("sp", out=io["ysT"][h], in_=y_t[:], reads=[y_r], dram_w=ys_res)
            if u["fh"]:
                prefetch(u["k"] + 1)

        run_pipeline(make_units("m"), [moba_s1, moba_s2, moba_s3], [0, 2, 3])

        z_ring = Ring([psf[0], psf[1], psf[6]])
        l_ring = Ring([psf[2], psf[5]])

        def sb_s1(u):
            q_t, q_r, k_t, k_r, v_t, v_r = hd[u["k"]][0:6]
            i, a, ap_ = u["i"], u["a"], u["ap"]
            sl = u["slot"]
            if u["first"]:
                sl["ot"] = ot_ring.next()
                sl["r_prev"] = None
                if i == 0:
                    hd[u["k"]] = hd[u["k"]][0:6] + (yst.next(),)
            qc = q_t[:, i * 512:(i + 1) * 512]
            zb, zbr = z_ring.next()
            terms = [(k_t[:, a * 128:(a + 1) * 128], qc, [k_r, q_r])]
            mm_group(P, zb[:], zbr, terms)
            u["z"] = (zb, zbr)

        def sb_s2(u):
            zb, zbr = u["z"]
            e_t, e_r = e1.next()
            fw.op("act", lambda: nc.scalar.activation(out=e_t[:], in_=zb[:], func=AF.Exp, scale=-1.0), reads=[zbr], writes=[e_r])
            sp_t, sp_r = spf.next()
            fw.op("act", lambda: nc.scalar.activation(out=sp_t[:], in_=e_t[:], func=AF.Ln, bias=1.0), reads=[e_r], writes=[sp_r])
            u["sp"] = (sp_t, sp_r)

        def sb_s3(u):
            zb, zbr = u["z"]
            sp_t, sp_r = u["sp"]
            lm_t, lm_r = lmb.next()
            ap_ = u["ap"]
            if mrange[0] <= ap_ < mrange[1]:
                l0, l0r = lm0.next()
                fw.op("dve", lambda: nc.vector.scalar_tensor_tensor(out=l0[:], in0=zb[:], scalar=-1.0, in1=sp_t[:], op0=ALU.mult, op1=ALU.subtract),
                      reads=[zbr, sp_r], writes=[l0r])
                fw.op("dve", lambda: nc.vector.tensor_tensor(out=lm_t[:], in0=l0[:], in1=mk[:, 128 * ap_:128 * ap_ + 512], op=ALU.mult),
                      reads=[l0r, mkr], writes=[lm_r])
            else:
                fw.op("dve", lambda: nc.vector.scalar_tensor_tensor(out=lm_t[:], in0=zb[:], scalar=-1.0, in1=sp_t[:], op0=ALU.mult, op1=ALU.subtract),
                      reads=[zbr, sp_r], writes=[lm_r])
            u["lm"] = (lm_t, lm_r)

        def sb_s4(u):
            sl = u["slot"]
            lm_t, lm_r = u["lm"]
            r_prev = sl["r_prev"]
            lb, lbr = l_ring.next()
            terms = [(tri[:], lm_t[:], [trir, lm_r])]
            if r_prev is not None:
                terms.append((ones[:], r_prev[0][:], [onesr, r_prev[1]]))
            mm_group(P, lb[:], lbr, terms)
            if not u["last"]:
                r_t, r_r = rbuf.next()
                if r_prev is None:
                    fw.op("pool", lambda: nc.gpsimd.tensor_copy(out=r_t[:], in_=lm_t[:]), reads=[lm_r], writes=[r_r])
                else:
                    fw.op("pool", lambda: nc.gpsimd.tensor_tensor(out=r_t[:], in0=r_prev[0][:], in1=lm_t[:], op=ALU.add),
                          reads=[r_prev[1], lm_r], writes=[r_r])
                sl["r_prev"] = (r_t, r_r)
            u["lb"] = (lb, lbr)

        def sb_s5(u):
            sp_t, sp_r = u["sp"]
            lb, lbr = u["lb"]
            tm, tmr = tmpf.next()
            fw.op("dve", lambda: nc.vector.tensor_tensor(out=tm[:], in0=lb[:], in1=sp_t[:], op=ALU.subtract),
                  reads=[lbr, sp_r], writes=[tmr])
            ap_ = u["ap"]
            if mrange[0] <= ap_ < mrange[1]:
                fw.op("dve", lambda: nc.vector.tensor_tensor(out=tm[:], in0=tm[:], in1=negk[:, 128 * ap_:128 * ap_ + 512], op=ALU.add),
                      reads=[tmr, negkr], writes=[tmr])
            u["tm"] = (tm, tmr)

        def sb_s6(u):
            tm, tmr = u["tm"]
            p_t, p_r = pbf.next()
            fw.op("act", lambda: nc.scalar.activation(out=p_t[:], in_=tm[:], func=AF.Exp), reads=[tmr], writes=[p_r])
            u["p"] = (p_t, p_r)

        def sb_s7(u):
            h = u["h"]
            q_t, q_r, k_t, k_r, v_t, v_r, (y_t, y_r) = hd[u["k"]]
            i, a = u["i"], u["a"]
            ot, otr = u["slot"]["ot"]
            p_t, p_r = u["p"]
            fw.op("pe", lambda: nc.tensor.matmul(ot[:], v_t[:, a, :], p_t[:], start=u["first"], stop=u["last"]),
                  reads=[v_r, p_r], writes=[otr])
            if u["last"]:
                fw.op("dve", lambda: nc.vector.tensor_copy(out=y_t[:, i * 512:(i + 1) * 512], in_=ot[:]), reads=[otr], writes=[y_r])
                if u["lh"]:
                    fw.dma("sp", out=io["ysT"][16 + h], in_=y_t[:], reads=[y_r], dram_w=ys_res)
            if u["fh"]:
                prefetch(u["k"] + 1)

        run_pipeline(make_units("s"), [sb_s1, sb_s2, sb_s3, sb_s4, sb_s5, sb_s6, sb_s7], [0, 1, 2, 3, 4, 5, 6])

        wp, wpr = P.sbr("wp", [128, 8, 256], BF16)
        fw.dma("pool", out=wp[:], in_=io["w_pool"].rearrange("g (k p) d -> p (g k) d", p=128), writes=[wpr])
        psc, pscr = P.sbr("psc", [128, 8], F32)
        fw.dma("sp", out=psc[:], in_=io["pscale"], writes=[pscr])
        rc0, rc0r = P.sbr("rc0", [128, 4 * 512], F32)
        fw.dma("sp", out=rc0[:], in_=io["rc0"], writes=[rc0r])
        halow, halowr = P.sbr("halow", [128, 2], F32)
        fw.dma("sp", out=halow[:], in_=io["halow"], writes=[halowr])
        ubuf = P.ring("ubuf", 3, [128, 528], F32)
        hal = P.ring("hal", 2, [128, 32], F32)
        sA = P.ring("sA", 2, [128, 528], F32)
        sB = P.ring("sB", 2, [128, 528], F32)
        pl = P.ring("pl", 4, [128, 512], BF16)
        ystb = P.ring("ystb", 3, [128, 512], BF16)
        for i in range(4 if do_pool else 0):
            for g in range(4):
                w = 2 << g
                pls = []
                for c2 in range(2):
                    cc = 2 * g + c2
                    u_t, u_r = ubuf.next()
                    fw.dma("sp", out=u_t[:, 16:528], in_=io["uT"][cc][:, i * 512:(i + 1) * 512], writes=[u_r])
                    h_t, h_r = hal.next()
                    fw.dma("sp", out=h_t[:, 0:16], in_=io["utail_g"][0, cc][:, i * 16:(i + 1) * 16], writes=[h_r])
                    if i > 0:
                        fw.dma("sp", out=h_t[:, 16:32], in_=io["utail_g"][1, cc][:, (i - 1) * 16:i * 16], writes=[h_r])
                    fw.op("dve", lambda: nc.vector.tensor_scalar(out=u_t[:, 0:16], in0=h_t[:, 0:16], scalar1=halow[:, 0:1], scalar2=None, op0=ALU.mult),
                          reads=[h_r, halowr, u_r], writes=[u_r])
                    if i > 0:
                        fw.op("dve", lambda: nc.vector.scalar_tensor_tensor(out=u_t[:, 0:16], in0=h_t[:, 16:32], scalar=halow[:, 1:2], in1=u_t[:, 0:16],
                                                                            op0=ALU.mult, op1=ALU.add), reads=[h_r, halowr, u_r], writes=[u_r])
                    a_t, a_r = sA.next()
                    b_t, b_r = sB.next()
                    fw.op("pool", lambda: nc.gpsimd.tensor_tensor(out=a_t[:, 1:528], in0=u_t[:, 1:528], in1=u_t[:, 0:527], op=ALU.add),
                          reads=[u_r], writes=[a_r])
                    cur, curr, oth, othr = a_t, a_r, b_t, b_r
                    sh = 2
                    lo = 1
                    while sh < w:
                        lo2 = lo + sh
                        fw.op("pool", lambda cur=cur, oth=oth, sh=sh, lo2=lo2: nc.gpsimd.tensor_tensor(
                            out=oth[:, lo2:528], in0=cur[:, lo2:528], in1=cur[:, lo2 - sh:528 - sh], op=ALU.add), reads=[curr], writes=[othr])
                        cur, curr, oth, othr = oth, othr, cur, curr
                        lo = lo2
                        sh *= 2
                    p_t, p_r = pl.next()
                    if i == 0:
                        fw.op("dve", lambda cur=cur: nc.vector.tensor_tensor(out=cur[:, 16:528], in0=cur[:, 16:528], in1=rc0[:, g * 512:(g + 1) * 512], op=ALU.mult),
                              reads=[curr, rc0r], writes=[curr])
                        fw.op("dve", lambda cur=cur: nc.vector.tensor_tensor(out=p_t[:], in0=cur[:, 16:528], in1=u_t[:, 16:528], op=ALU.subtract),
                              reads=[curr, u_r], writes=[p_r])
                    else:
                        fw.op("dve", lambda cur=cur: nc.vector.scalar_tensor_tensor(out=p_t[:], in0=cur[:, 16:528], scalar=1.0 / w, in1=u_t[:, 16:528],
                                                                                    op0=ALU.mult, op1=ALU.subtract), reads=[curr, u_r], writes=[p_r])
                    pls.append((p_t, p_r))
                for db in range(2):
                    bank, br = s_ring.next()
                    mm_group(P, bank[:], br, [(wp[:, 2 * g + kk, db * 128:(db + 1) * 128], pls[kk][0][:], [wpr, pls[kk][1]]) for kk in range(2)])
                    yb_t, yb_r = ystb.next()
                    cc = 2 * g + db
                    fw.op("dve", lambda: nc.vector.tensor_scalar(out=yb_t[:], in0=bank[:], scalar1=psc[:, cc:cc + 1], scalar2=None, op0=ALU.mult),
                          reads=[br, pscr], writes=[yb_r])
                    fw.dma("sp", out=io["ysT"][8 + cc][:, i * 512:(i + 1) * 512], in_=yb_t[:], reads=[yb_r], dram_w=ys_res)
        P.end_phase()
    P.stack = None

    if not do_chunk:
        return
    with contextlib.ExitStack() as st:
        P.stack = st
        ident, identr = P.sbr("identC", [128, 128], BF16)
        fw.dma("sp", out=ident[:], in_=io["ident"], writes=[identr])
        gffn, gffnr = P.sbr("gffn", [128, D], F32)
        fw.dma("sp", out=gffn[:], in_=io["gffn"], writes=[gffnr])
        if last:
            gfin, gfinr = P.sbr("gfin", [128, D], F32)
            fw.dma("sp", out=gfin[:], in_=io["gfin"], writes=[gfinr])
        xc = P.sb("xc", [128, KC, 512], BF16)
        xcr = P.res("xc")
        big = P.sb("big", [128, FC, 512], BF16)
        ycr = P.res("yc")
        mcr = [P.res("mc%d" % c) for c in range(16)]
        atr = [P.res("at%d" % c) for c in range(FC)]
        for r_ in [ycr] + mcr:
            r_.alias = list(atr)
        for r_ in atr:
            r_.alias = [ycr] + mcr
        hn = [P.sbr("hn%d" % t, [128, D], F32) for t in range(4)]
        wslots = [P.sbr("wrc%d" % i, [128, 8, 512], BF16) for i in range(NRING)]
        sig = P.ring("sig", 2, [128, 512], F32)
        prod = P.ring("prod", 2, [128, 512], F32)
        macc = [P.sbr("macc%d" % c, [128, 512], F32) for c in range(4)]
        xn, xnr = P.sbr("xnC", [128, D], BF16)
        junk, junkr = P.sbr("junkC", [128, D], BF16)
        stat = P.sb("statC", [128, 64], F32)
        statr = [P.res("statC%d" % t) for t in range(8)]
        ob = P.ring("ob", 1, [128, D], F32) if last else None
        psf = Ring([(P.ps("psC%d" % i, [128, 512], F32), P.res("psC%d" % i)) for i in range(7)])
        psb, psbr = P.ps("psCb", [128, 1024], BF16), P.res("psCb")

        W = {k: io[k] for k in ("w_in", "w_br_a", "w_br_b", "w_br_c", "w_out", "w_gate", "w_up", "w_down")}
        wbr = [W["w_br_a"], W["w_br_b"], W["w_br_c"]]
        plist = []
        for s in range(4):
            for cg in range(4):
                for br_ in range(3):
                    c0 = 7168 + br_ * 2048 + cg * 512
                    plist.append((W["w_in"], 0, 8, c0, 512))
                    plist.append((W["w_in"], 8, 8, c0, 512))
                    plist.append((wbr[br_], 0, 8, cg * 512, 512))
            for cg in range(4):
                plist.append((W["w_out"], 0, 8, cg * 512, 512))
                plist.append((W["w_out"], 8, 8, cg * 512, 512))
            for fg in range(11):
                plist.append((W["w_gate"], 0, 8, fg * 512, 512))
                plist.append((W["w_gate"], 8, 8, fg * 512, 512))
                plist.append((W["w_up"], 0, 8, fg * 512, 512))
                plist.append((W["w_up"], 8, 8, fg * 512, 512))
            for cg in range(4):
                for pc in range(6):
                    nkp = 8 if pc < 5 else 4
                    plist.append((W["w_down"], pc * 8, nkp, cg * 512, 512))
        ws = WStream(P, wslots, plist)

        def rmsnorm_tile(src_t, src_r, sidx, gain_t, gain_r, out_t, out_r):
            c = sidx * 4
            sr = statr[sidx]
            fw.op("act", lambda: nc.scalar.activation(out=junk[:], in_=src_t[:], func=AF.Square, accum_out=stat[:, c:c + 1]),
                  reads=[src_r], writes=[junkr, sr])
            fw.op("dve", lambda: nc.vector.tensor_scalar(out=stat[:, c + 1:c + 2], in0=stat[:, c:c + 1], scalar1=1.0 / D, scalar2=EPS,
                                                         op0=ALU.mult, op1=ALU.add), reads=[sr], writes=[sr])
            fw.op("act", lambda: nc.scalar.activation(out=stat[:, c + 2:c + 3], in_=stat[:, c + 1:c + 2], func=AF.Sqrt), reads=[sr], writes=[sr])
            fw.op("dve", lambda: nc.vector.reciprocal(out=stat[:, c + 3:c + 4], in_=stat[:, c + 2:c + 3]), reads=[sr], writes=[sr])
            fw.op("dve", lambda: nc.vector.scalar_tensor_tensor(out=out_t[:], in0=src_t[:], scalar=stat[:, c + 3:c + 4], in1=gain_t[:],
                                                                op0=ALU.mult, op1=ALU.mult), reads=[src_r, sr, gain_r], writes=[out_r])

        for s in range(4):
            tok0 = s * 512
            fw.dma("sp", out=xc[:], in_=io["xT"][:, :, tok0:tok0 + 512].rearrange("k p t -> p k t"), writes=[xcr])
            fw.dma("sp", out=big[:, 0:24, :], in_=io["ysT"][:, :, tok0:tok0 + 512].rearrange("k p t -> p k t"), reads=[ys_res], writes=[ycr], semres=ycr)
            ycr.drd = []
            for tt in range(4):
                fw.dma("sp", out=hn[tt][0][:], in_=io["h"][tok0 + tt * 128:tok0 + (tt + 1) * 128, :], writes=[hn[tt][1]])
            for cg in range(4):
                for br_ in range(3):
                    g0, g1, bw = ws.take(3)
                    for cb in range(4):
                        gbank, gbr = psf.next()
                        terms = []
                        for k in range(16):
                            sl, slr = (g0, g1)[k // 8]
                            terms.append((sl[:, k % 8, cb * 128:(cb + 1) * 128], xc[:, k, :], [slr, xcr]))
                        mm_group(P, gbank[:], gbr, terms)
                        bbank, bbr = psf.next()
                        terms = []
                        for k in range(8):
                            terms.append((bw[0][:, k, cb * 128:(cb + 1) * 128], big[:, br_ * 8 + k, :], [bw[1], ycr]))
                        mm_group(P, bbank[:], bbr, terms)
                        sg, sgr = sig.next()
                        fw.op("act", lambda: nc.scalar.activation(out=sg[:], in_=gbank[:], func=AF.Sigmoid), reads=[gbr], writes=[sgr])
                        mc_t, mc_r = macc[cb]
                        if br_ == 0:
                            fw.op("dve", lambda: nc.vector.tensor_tensor(out=mc_t[:], in0=sg[:], in1=bbank[:], op=ALU.mult),
                                  reads=[sgr, bbr], writes=[mc_r])
                        else:
                            pr, prr = prod.next()
                            fw.op("dve", lambda: nc.vector.tensor_tensor(out=pr[:], in0=sg[:], in1=bbank[:], op=ALU.mult),
                                  reads=[sgr, bbr], writes=[prr])
                            if br_ == 1:
                                fw.op("dve", lambda: nc.vector.tensor_tensor(out=mc_t[:], in0=mc_t[:], in1=pr[:], op=ALU.add),
                                      reads=[mc_r, prr], writes=[mc_r])
                            else:
                                col = cg * 4 + cb
                                fw.op("dve", lambda: nc.vector.tensor_tensor(out=big[:, 24 + col, :], in0=mc_t[:], in1=pr[:], op=ALU.add),
                                      reads=[mc_r, prr], writes=[mcr[col]])
                    ws.release(3)
            for cg in range(4):
                w0, w1 = ws.take(2)
                for tt in range(4):
                    bank, br = psf.next()
                    terms = []
                    for k in range(16):
                        sl, slr = (w0, w1)[k // 8]
                        terms.append((big[:, 24 + k, tt * 128:(tt + 1) * 128], sl[:, k % 8, :], [slr, mcr[k]]))
                    mm_group(P, bank[:], br, terms)
                    h_t, h_r = hn[tt]
                    fw.op("dve", lambda: nc.vector.tensor_tensor(out=h_t[:, cg * 512:(cg + 1) * 512], in0=h_t[:, cg * 512:(cg + 1) * 512], in1=bank[:], op=ALU.add),
                          reads=[br, h_r], writes=[h_r])
                ws.release(2)
            for tt in range(4):
                h_t, h_r = hn[tt]
                rmsnorm_tile(h_t, h_r, tt, gffn, gffnr, xn, xnr)
                for half in range(2):
                    for j in range(8):
                        k = half * 8 + j
                        fw.op("pe", lambda: nc.tensor.transpose(psb[:, j * 128:(j + 1) * 128], xn[:, k * 128:(k + 1) * 128], ident[:]),
                              reads=[xnr, identr], writes=[psbr], inc=(j == 7))
                    dst = xc[:, half * 8:(half + 1) * 8, tt * 128:(tt + 1) * 128]
                    src = psb[:, :].rearrange("p (a b) -> p a b", b=128)
                    if half == 0:
                        fw.op("act", lambda: nc.scalar.copy(out=dst, in_=src), reads=[psbr], writes=[xcr])
                    else:
                        fw.op("dve", lambda: nc.vector.tensor_copy(out=dst, in_=src), reads=[psbr], writes=[xcr])
            for fg in range(11):
                g0, g1, u0, u1 = ws.take(4)
                for fb in range(4):
                    gbank, gbr = psf.next()
                    terms = []
                    for k in range(16):
                        sl, slr = (g0, g1)[k // 8]
                        terms.append((sl[:, k % 8, fb * 128:(fb + 1) * 128], xc[:, k, :], [slr, xcr]))
                    mm_group(P, gbank[:], gbr, terms)
                    ubank, ubr = psf.next()
                    terms = []
                    for k in range(16):
                        sl, slr = (u0, u1)[k // 8]
                        terms.append((sl[:, k % 8, fb * 128:(fb + 1) * 128], xc[:, k, :], [slr, xcr]))
                    mm_group(P, ubank[:], ubr, terms)
                    sg, sgr = sig.next()
                    fw.op("act", lambda: nc.scalar.activation(out=sg[:], in_=gbank[:], func=AF.Silu), reads=[gbr], writes=[sgr])
                    fc = fg * 4 + fb
                    fw.op("dve", lambda: nc.vector.tensor_tensor(out=big[:, fc, :], in0=sg[:], in1=ubank[:], op=ALU.mult),
                          reads=[sgr, ubr], writes=[atr[fc]])
                ws.release(4)
            for cg in range(4):
                banks = [psf.next() for _ in range(4)]
                for pc in range(6):
                    nkp = 8 if pc < 5 else 4
                    if pc > 0:
                        ws.release(1)
                    (sl, slr), = ws.take(1)
                    for tt in range(4):
                        bank, br = banks[tt]
                        for kk in range(nkp):
                            k = pc * 8 + kk
                            fw.op("pe", lambda: nc.tensor.matmul(bank[:], big[:, k, tt * 128:(tt + 1) * 128], sl[:, kk, :],
                                                                 start=(k == 0), stop=(k == FC - 1)),
                                  reads=[slr, atr[k]], writes=[br], inc=(kk == nkp - 1))
                ws.release(1)
                for tt in range(4):
                    bank, br = banks[tt]
                    h_t, h_r = hn[tt]
                    fw.op("dve", lambda: nc.vector.tensor_tensor(out=h_t[:, cg * 512:(cg + 1) * 512], in0=h_t[:, cg * 512:(cg + 1) * 512], in1=bank[:], op=ALU.add),
                          reads=[br, h_r], writes=[h_r])
            for tt in range(4):
                h_t, h_r = hn[tt]
                dst = io["hout"][tok0 + tt * 128:tok0 + (tt + 1) * 128, :]
                if last:
                    o_t, o_r = ob.next()
                    rmsnorm_tile(h_t, h_r, 4 + tt, gfin, gfinr, o_t, o_r)
                    fw.dma("sp", out=dst, in_=o_t[:], reads=[o_r])
                else:
                    fw.dma("sp", out=dst, in_=h_t[:], reads=[h_r])
        P.end_phase()
    P.stack = None


def _bucket(d):
    n = np.maximum(d, 0)
    nf = np.maximum(n, 1).astype(np.float32)
    large = 16 + (np.log(nf / np.float32(16)) / np.float32(math.log(2048 / 16)) * np.float32(16)).astype(np.int32)
    large = np.minimum(large, 31)
    return np.where(n < 16, n, large)


def _core_tables(j, rel_bias):
    c0 = 512 * j - 896
    k = np.arange(128)[:, None]
    m = np.arange(TKW)[None, :]
    d = m - k + c0
    bk = _bucket(d)
    tkt = rel_bias[:, bk]
    tkt = np.where((d >= 0)[None], tkt, np.float32(NEGV)).astype(np.float32)
    m2 = np.arange(MKW)[None, :]
    d2 = m2 - k + c0
    mk = (d2 >= 1).astype(np.float32)
    negk = np.where(d2 >= 1, 0.0, NEGV).astype(np.float32)
    pastneg = np.zeros((16, 16), np.float32)
    past01 = np.zeros((16, 16), np.float32)
    own01 = np.zeros((16, 16), np.float32)
    for t in range(16):
        i, r = t // 4, t % 4
        own = 4 * i + 2 * j + r // 2
        for n in range(16):
            pastneg[t, n] = 0.0 if n < own else -1e30
            past01[t, n] = 1.0 if n < own else 0.0
            own01[t, n] = 1.0 if n == own else 0.0
    bc = lambda a: np.ascontiguousarray(np.broadcast_to(a.reshape(1, -1), (128, a.size))).astype(np.float32)
    rc0 = np.zeros((4, 512), np.float32)
    for g in range(4):
        w = 2 << g
        pos = 512 * j + np.arange(512)
        rc0[g] = 1.0 / np.minimum(pos + 1, w)
    halow = np.array([1.0, 0.0] if j == 1 else [0.0, 1.0], np.float32)
    return {
        "TK": np.ascontiguousarray(tkt),
        "mk": mk.astype(ml_dtypes.bfloat16),
        "negk": negk,
        "pastneg": bc(pastneg), "past01": bc(past01), "own01": bc(own01),
        "bfar": bc(rel_bias[:, 31]),
        "rc0": bc(rc0), "halow": bc(halow),
    }


def _consts():
    ident = np.eye(128, dtype=np.float32).astype(ml_dtypes.bfloat16)
    jj = np.arange(128)[:, None]
    ss = np.arange(128)[None, :]
    tri = (jj > ss).astype(np.float32).astype(ml_dtypes.bfloat16)
    emat = np.zeros((16, 16, 128), np.float32)
    for n in range(16):
        emat[n, n, :] = 1.0
    return {"ident": ident, "tri": tri, "emat": emat.reshape(16, 2048).astype(ml_dtypes.bfloat16)}


def _decl(nc, name, shape, dt, kind):
    return nc.dram_tensor(name, list(shape), dt, kind=kind).ap()


A_IN = {"h": ([T, D], F32), "w_in": ([D, INW], F32), "gmix": ([128, D], F32), "ident": ([128, 128], BF16)}
A_OUT = {"xT": ([KC, 128, T], BF16), "qT": ([16, 128, T], BF16), "kT": ([16, 128, T], BF16), "vtm": ([T, D], BF16),
         "uT": ([8, 128, T], F32), "utail": ([8, 128, 64], F32), "kbarT": ([8, 128, 8], F32)}
B_IN = {"h": ([T, D], F32), "xT": ([KC, 128, T], BF16), "qT": ([16, 128, T], BF16), "uT": ([8, 128, T], F32),
        "kT_g": ([2, 16, 128, T], BF16), "vtm_g": ([2, T, D], BF16), "kbar_g": ([2, 8, 128, 8], F32),
        "utail_g": ([2, 8, 128, 64], F32),
        "w_in": ([D, INW], F32), "w_pool": ([4, 256, 256], F32), "pscale": ([128, 8], F32),
        "w_br_a": ([1024, D], F32), "w_br_b": ([1024, D], F32), "w_br_c": ([1024, D], F32), "w_out": ([D, D], F32),
        "w_gate": ([D, DFF], F32), "w_up": ([D, DFF], F32), "w_down": ([DFF, D], F32),
        "gffn": ([128, D], F32), "gfin": ([128, D], F32),
        "TK": ([8, 128, TKW], F32), "bfar": ([128, 8], F32), "mk": ([128, MKW], BF16), "negk": ([128, MKW], F32),
        "pastneg": ([128, 256], F32), "past01": ([128, 256], F32), "own01": ([128, 256], F32),
        "halow": ([128, 2], F32), "rc0": ([128, 2048], F32),
        "ident": ([128, 128], BF16), "tri": ([128, 128], BF16), "emat": ([16, 2048], BF16)}


def build_blend(P, io):
    nc, fw = P.nc, P.fw
    with contextlib.ExitStack() as st:
        P.stack = st
        sw, swr = P.sbr("selw", [128, 2], F32)
        fw.dma("sp", out=sw[:], in_=io["selw"], writes=[swr])
        ra = P.ring("bla", 2, [128, D], F32)
        rb = P.ring("blb", 2, [128, D], F32)
        ro = P.ring("blo", 2, [128, D], F32)
        for tt in range(16):
            a_t, a_r = ra.next()
            b_t, b_r = rb.next()
            o_t, o_r = ro.next()
            fw.dma("sp", out=a_t[:], in_=io["a"][tt * 128:(tt + 1) * 128, :], writes=[a_r])
            fw.dma("sp", out=b_t[:], in_=io["b"][tt * 128:(tt + 1) * 128, :], writes=[b_r])
            fw.op("dve", lambda: nc.vector.tensor_scalar(out=o_t[:], in0=a_t[:], scalar1=sw[:, 0:1], scalar2=None, op0=ALU.mult),
                  reads=[a_r, swr], writes=[o_r])
            fw.op("dve", lambda: nc.vector.scalar_tensor_tensor(out=o_t[:], in0=b_t[:], scalar=sw[:, 1:2], in1=o_t[:], op0=ALU.mult, op1=ALU.add),
                  reads=[b_r, swr, o_r], writes=[o_r])
            fw.dma("sp", out=io["out"][tt * 128:(tt + 1) * 128, :], in_=o_t[:], reads=[o_r])
        P.end_phase()
    P.stack = None


TAB_KEYS = {"TK": ([8, 128, TKW], F32), "bfar": ([128, 8], F32), "mk": ([128, MKW], BF16), "negk": ([128, MKW], F32),
            "pastneg": ([128, 256], F32), "past01": ([128, 256], F32), "own01": ([128, 256], F32),
            "halow": ([128, 2], F32), "rc0": ([128, 2048], F32)}
W_KEYS = {"w_in": [2, D, INW], "w_pool": [2, 4, 256, 256], "w_br_a": [2, 1024, D], "w_br_b": [2, 1024, D], "w_br_c": [2, 1024, D],
          "w_out": [2, D, D], "w_gate": [2, D, DFF], "w_up": [2, D, DFF], "w_down": [2, DFF, D]}


def _prog_fused():
    nc = bass.Bass("TRN2", target_bir_lowering=False)
    gi = {}
    for k in ("x0", "x1"):
        gi[k] = _decl(nc, k, [T, D], F32, "ExternalInput")
    for k, shp in W_KEYS.items():
        gi[k] = _decl(nc, k, shp, F32, "ExternalInput")
    for k in ("gmix0", "gmix1", "gffn0", "gffn1", "gfin"):
        gi[k] = _decl(nc, k, [128, D], F32, "ExternalInput")
    for k in ("pscale0", "pscale1"):
        gi[k] = _decl(nc, k, [128, 8], F32, "ExternalInput")
    for pre in ("t0_", "t1_", "to_"):
        for k, (shp, dt) in TAB_KEYS.items():
            gi[pre + k] = _decl(nc, pre + k, shp, dt, "ExternalInput")
    gi["selw"] = _decl(nc, "selw", [128, 2], F32, "ExternalInput")
    gi["ident"] = _decl(nc, "ident", [128, 128], BF16, "ExternalInput")
    gi["tri"] = _decl(nc, "tri", [128, 128], BF16, "ExternalInput")
    gi["emat"] = _decl(nc, "emat", [16, 2048], BF16, "ExternalInput")
    hout = _decl(nc, "hout", [T, D], F32, "ExternalOutput")
    xT_s = _decl(nc, "xT_s", [3, KC, 128, T], BF16, "Internal")
    qT_s = _decl(nc, "qT_s", [3, 16, 128, T], BF16, "Internal")
    uT_s = _decl(nc, "uT_s", [3, 8, 128, T], F32, "Internal")
    kT_s = _decl(nc, "kT_s", [2, 16, 128, T], BF16, "Internal")
    vtm_s = _decl(nc, "vtm_s", [2, T, D], BF16, "Internal")
    kbar_s = _decl(nc, "kbar_s", [2, 8, 128, 8], F32, "Internal")
    utail_s = _decl(nc, "utail_s", [2, 8, 128, 64], F32, "Internal")
    ysT = _decl(nc, "ysT_s", [24, 128, T], BF16, "Internal")
    h1_s = _decl(nc, "h1_s", [3, T, D], F32, "Internal")
    P = Prog(nc)

    def a_io(h, l, slot, kvslot):
        d = {"h": h, "w_in": gi["w_in"][l], "gmix": gi["gmix%d" % l], "ident": gi["ident"],
             "xT": xT_s[slot], "qT": qT_s[slot], "uT": uT_s[slot]}
        if kvslot is not None:
            d.update({"kT": kT_s[kvslot], "vtm": vtm_s[kvslot], "utail": utail_s[kvslot], "kbarT": kbar_s[kvslot]})
        return d

    def b_io(h, l, slot, tabpre, out):
        d = {"h": h, "xT": xT_s[slot], "qT": qT_s[slot], "uT": uT_s[slot], "kT_g": kT_s, "vtm_g": vtm_s,
             "kbar_g": kbar_s, "utail_g": utail_s, "w_in": gi["w_in"][l], "w_pool": gi["w_pool"][l],
             "pscale": gi["pscale%d" % l], "w_br_a": gi["w_br_a"][l], "w_br_b": gi["w_br_b"][l], "w_br_c": gi["w_br_c"][l],
             "w_out": gi["w_out"][l], "w_gate": gi["w_gate"][l], "w_up": gi["w_up"][l], "w_down": gi["w_down"][l],
             "gffn": gi["gffn%d" % l], "gfin": gi["gfin"], "ident": gi["ident"], "tri": gi["tri"], "emat": gi["emat"],
             "hout": out, "ysT": ysT}
        for k in TAB_KEYS:
            d[k] = gi[tabpre + k]
        return d

    build_A(P, a_io(gi["x0"], 0, 0, 0))
    build_A(P, a_io(gi["x1"], 0, 1, 1))
    build_B(P, b_io(gi["x0"], 0, 0, "t0_", h1_s[0]), False, skip0=4, mrange=(4, 8))
    build_B(P, b_io(gi["x1"], 0, 1, "t1_", h1_s[1]), False, mrange=(0, 4))
    build_A(P, a_io(h1_s[0], 1, 0, 0), do_q=False)
    build_A(P, a_io(h1_s[1], 1, 1, 1), do_q=False)
    build_blend(P, {"a": h1_s[0], "b": h1_s[1], "out": h1_s[2], "selw": gi["selw"]})
    build_A(P, a_io(h1_s[2], 1, 2, None), do_kv=False)
    build_B(P, b_io(h1_s[2], 1, 2, "to_", hout), True)
    return nc


def _bcast(v):
    return np.ascontiguousarray(np.broadcast_to(np.asarray(v, np.float32).reshape(1, -1), (128, v.size)))


def kernel(x, norm_mix, norm_ffn, w_in, w_pool, pool_scale, w_br_a, w_br_b, w_br_c,
           w_out, w_gate, w_up, w_down, rel_bias, norm_final):
    f = lambda a: np.ascontiguousarray(np.asarray(a, dtype=np.float32))
    x = f(x)
    norm_mix, norm_ffn, pool_scale, rel_bias, norm_final = f(norm_mix), f(norm_ffn), f(pool_scale), f(rel_bias), f(norm_final)
    wts = {"w_in": f(w_in), "w_pool": f(w_pool), "w_br_a": f(w_br_a), "w_br_b": f(w_br_b), "w_br_c": f(w_br_c),
           "w_out": f(w_out), "w_gate": f(w_gate), "w_up": f(w_up), "w_down": f(w_down)}
    cores = list(range(8))
    consts = _consts()
    tabs = [_core_tables(j, rel_bias) for j in range(2)]
    common = dict(wts)
    common.update(consts)
    for l in range(2):
        common["gmix%d" % l] = _bcast(norm_mix[l])
        common["gffn%d" % l] = _bcast(norm_ffn[l])
        common["pscale%d" % l] = np.ascontiguousarray(pool_scale[l].reshape(8, 128).T)
    common["gfin"] = _bcast(norm_final)
    for r in range(2):
        for k in TAB_KEYS:
            common["t%d_%s" % (r, k)] = tabs[r][k]
    in_maps = []
    for c in cores:
        b, j = c // 2, c % 2
        d = dict(common)
        xb = x[b].reshape(8, 512, D)
        d["x0"] = np.ascontiguousarray(xb[0::2].reshape(T, D))
        d["x1"] = np.ascontiguousarray(xb[1::2].reshape(T, D))
        for k in TAB_KEYS:
            d["to_" + k] = tabs[j][k]
        sel = np.zeros((128, 2), np.float32)
        sel[:, j] = 1.0
        d["selw"] = sel
        in_maps.append(d)
    nc = _prog_fused()
    res = run_bass_kernel_spmd(nc, in_maps, core_ids=cores).results
    out = np.empty((4, 4096, D), np.float32)
    for c in cores:
        b, j = c // 2, c % 2
        out[b].reshape(8, 512, D)[j::2] = np.asarray(res[c]["hout"]).reshape(4, 512, D)
    return out
```
